# Optimizing a Trainium2 kernel written in Bass

```python
import math
import jax, jax.numpy as jnp
from jax import lax
import numpy as np

D_MODEL = 1024
BATCH = 8
SEQ = 2048
DEPTH = 1

CTX_LEN = 256
GRID_W = 64
HEAD_DIM = 64
ATTN_HEADS = 8
KV_HEADS = 2
Q_PER_KV = ATTN_HEADS // KV_HEADS
ATTN_WIDTH = ATTN_HEADS * HEAD_DIM
KV_WIDTH = KV_HEADS * HEAD_DIM
WINDOW = 128
BLOCK = 128
ROPE_BASE = 10000.0
SSM_WIDTH = D_MODEL - ATTN_WIDTH
SSM_HEAD_DIM = 64
SSM_HEADS = SSM_WIDTH // SSM_HEAD_DIM
SSM_GROUPS = 2
D_STATE = 128
XBC_WIDTH = SSM_WIDTH + 2 * SSM_GROUPS * D_STATE
SSM_CONV = 3
CHUNK = 128
D_FF = 2816
FFN_CONV = 3
IN_COLS = ATTN_WIDTH + 2 * KV_WIDTH + SSM_WIDTH + XBC_WIDTH + 2 * SSM_HEADS
EPS = 1e-6
NEG_INF = -1e30

kernel_name = 'hybrid_swa_ssd_dit_block'


def rmsnorm(x, g):
    xf = x.astype(jnp.float32)
    y = xf * lax.rsqrt(jnp.mean(xf * xf, axis=-1, keepdims=True) + EPS)
    return y.astype(x.dtype) * g


def modulate(h, shift, scale):
    return h * (1 + scale) + shift


def dwconv_centred(u, w, b):
    out = lax.conv_general_dilated(u, w[:, None, :], window_strides=(1,), padding='SAME',
                                   dimension_numbers=('NWC', 'WIO', 'NWC'),
                                   feature_group_count=u.shape[-1])
    return out + b


def axial_rope_tables(L):
    rows = L // GRID_W
    pos_r = jnp.repeat(jnp.arange(rows), GRID_W).astype(jnp.float32)
    pos_c = jnp.tile(jnp.arange(GRID_W), rows).astype(jnp.float32)
    quarter = HEAD_DIM // 4
    freqs = ROPE_BASE ** (-jnp.arange(quarter, dtype=jnp.float32) / quarter)
    ang_r = pos_r[:, None] * freqs[None, :]
    ang_c = pos_c[:, None] * freqs[None, :]
    return (jnp.cos(ang_r), jnp.sin(ang_r), jnp.cos(ang_c), jnp.sin(ang_c))


def _rotate(x, cos, sin):
    x1, x2 = jnp.split(x, 2, axis=-1)
    cos = cos[:, None, :].astype(x.dtype)
    sin = sin[:, None, :].astype(x.dtype)
    return jnp.concatenate([x1 * cos - x2 * sin, x2 * cos + x1 * sin], axis=-1)


def apply_axial_rope(x, tables):
    cos_r, sin_r, cos_c, sin_c = tables
    xr, xc = jnp.split(x, 2, axis=-1)
    return jnp.concatenate([_rotate(xr, cos_r, sin_r), _rotate(xc, cos_c, sin_c)], axis=-1)


def split_projection(p):
    sizes = [ATTN_WIDTH, KV_WIDTH, KV_WIDTH, SSM_WIDTH, XBC_WIDTH]
    points, acc = [], 0
    for s in sizes:
        acc += s
        points.append(acc)
    return jnp.split(p, points, axis=-1)


def to_heads(t, n):
    return t.reshape(t.shape[0], t.shape[1], n, HEAD_DIM)


def window_attention(q, k, v, k_ctx, v_ctx, sink):
    b, L = q.shape[0], q.shape[1]
    lc = k_ctx.shape[1]
    nb = L // BLOCK
    scale = HEAD_DIM ** -0.5
    qb = q.reshape(b, nb, BLOCK, KV_HEADS, Q_PER_KV, HEAD_DIM)

    def banded(t):
        tp = jnp.pad(t, ((0, 0), (BLOCK, BLOCK), (0, 0), (0, 0)))
        tp = tp.reshape(b, nb + 2, BLOCK, KV_HEADS, HEAD_DIM)
        return jnp.concatenate([tp[:, :-2], tp[:, 1:-1], tp[:, 2:]], axis=2)

    kw, vw = banded(k), banded(v)
    s_loc = jnp.einsum('bnqkgd,bnjkd->bnkgqj', qb, kw).astype(jnp.float32) * scale
    blk = jnp.arange(nb)[:, None, None]
    qi = jnp.arange(BLOCK)[None, :, None]
    kj = jnp.arange(3 * BLOCK)[None, None, :]
    q_pos = blk * BLOCK + qi
    k_pos = blk * BLOCK - BLOCK + kj
    valid = (jnp.abs(k_pos - q_pos) <= WINDOW) & (k_pos >= 0) & (k_pos < L)
    s_loc = jnp.where(valid[None, :, None, None], s_loc, NEG_INF)
    s_ctx = jnp.einsum('bnqkgd,bckd->bnkgqc', qb, k_ctx).astype(jnp.float32) * scale
    s_sink = jnp.broadcast_to(
        sink.astype(jnp.float32).reshape(KV_HEADS, Q_PER_KV)[None, None, :, :, None, None],
        s_loc.shape[:-1] + (1,))
    p = jax.nn.softmax(jnp.concatenate([s_loc, s_ctx, s_sink], axis=-1), axis=-1).astype(v.dtype)
    p_loc = p[..., :3 * BLOCK]
    p_ctx = p[..., 3 * BLOCK:3 * BLOCK + lc]
    out = (jnp.einsum('bnkgqj,bnjkd->bnqkgd', p_loc, vw)
           + jnp.einsum('bnkgqc,bckd->bnqkgd', p_ctx, v_ctx))
    return out.reshape(b, L, ATTN_WIDTH)


def context_attention(q_c, k_c, v_c, sink):
    b, lc = q_c.shape[0], q_c.shape[1]
    scale = HEAD_DIM ** -0.5
    qg = q_c.reshape(b, lc, KV_HEADS, Q_PER_KV, HEAD_DIM)
    s = jnp.einsum('bqkgd,bckd->bkgqc', qg, k_c).astype(jnp.float32) * scale
    s_sink = jnp.broadcast_to(
        sink.astype(jnp.float32).reshape(KV_HEADS, Q_PER_KV)[None, :, :, None, None], s.shape[:-1] + (1,))
    p = jax.nn.softmax(jnp.concatenate([s, s_sink], axis=-1), axis=-1)[..., :lc].astype(v_c.dtype)
    out = jnp.einsum('bkgqc,bckd->bqkgd', p, v_c)
    return out.reshape(b, lc, ATTN_WIDTH)


def ssd_scan(xs, dt, A, Bm, Cm, h0):
    b, L = xs.shape[0], xs.shape[1]
    nc = L // CHUNK
    xc = xs.reshape(b, nc, CHUNK, SSM_HEADS, SSM_HEAD_DIM)
    dtc = dt.reshape(b, nc, CHUNK, SSM_HEADS)
    Bc = Bm.reshape(b, nc, CHUNK, SSM_HEADS, D_STATE)
    Cc = Cm.reshape(b, nc, CHUNK, SSM_HEADS, D_STATE)
    acum = jnp.cumsum(dtc * A, axis=2)
    seg = acum[:, :, :, None, :] - acum[:, :, None, :, :]
    lower = jnp.tril(jnp.ones((CHUNK, CHUNK), dtype=bool))[None, None, :, :, None]
    Lmat = jnp.where(lower, jnp.exp(jnp.where(lower, seg, 0.0)), 0.0)
    xdt = xc * dtc[..., None]
    scores = jnp.einsum('bcihn,bcjhn->bcijh', Cc, Bc) * Lmat
    y_diag = jnp.einsum('bcijh,bcjhp->bcihp', scores, xdt)
    decay_end = jnp.exp(acum[:, :, -1:, :] - acum)
    states = jnp.einsum('bcjhn,bcjh,bcjhp->bchpn', Bc, decay_end, xdt)
    chunk_decay = jnp.exp(acum[:, :, -1, :])

    def step(h, inp):
        dec, st = inp
        return dec[:, :, None, None] * h + st, h

    h_final, h_start = lax.scan(step, h0, (jnp.moveaxis(chunk_decay, 1, 0), jnp.moveaxis(states, 1, 0)))
    h_start = jnp.moveaxis(h_start, 0, 1)
    y_off = jnp.einsum('bcihn,bchpn,bcih->bcihp', Cc, h_start, jnp.exp(acum))
    y = (y_diag + y_off).reshape(b, L, SSM_HEADS, SSM_HEAD_DIM)
    return y, h_final


def ssm_mixer(z, xbc, dt_raw, conv_w, conv_b, a_log, dt_bias, d_skip, g_ssm, h0_fwd, h0_bwd):
    b, L = z.shape[0], z.shape[1]
    xbc = jax.nn.silu(dwconv_centred(xbc, conv_w, conv_b)).astype(jnp.float32)
    xs, Bm, Cm = jnp.split(xbc, [SSM_WIDTH, SSM_WIDTH + SSM_GROUPS * D_STATE], axis=-1)
    xs = xs.reshape(b, L, SSM_HEADS, SSM_HEAD_DIM)
    hpg = SSM_HEADS // SSM_GROUPS
    Bm = jnp.repeat(Bm.reshape(b, L, SSM_GROUPS, D_STATE), hpg, axis=2)
    Cm = jnp.repeat(Cm.reshape(b, L, SSM_GROUPS, D_STATE), hpg, axis=2)
    dt_raw = dt_raw.astype(jnp.float32)
    dt_bias = dt_bias.astype(jnp.float32)
    dt_f = jax.nn.softplus(dt_raw[..., :SSM_HEADS] + dt_bias[0])
    dt_b = jax.nn.softplus(dt_raw[..., SSM_HEADS:] + dt_bias[1])
    A = -jnp.exp(a_log.astype(jnp.float32))
    y_f, h_f = ssd_scan(xs, dt_f, A[0], Bm, Cm, h0_fwd)
    flip = lambda t: jnp.flip(t, axis=1)
    y_b, h_b = ssd_scan(flip(xs), flip(dt_b), A[1], flip(Bm), flip(Cm), h0_bwd)
    y = y_f + flip(y_b) + d_skip.astype(jnp.float32)[:, None] * xs
    y = y.reshape(b, L, SSM_WIDTH) * jax.nn.silu(z.astype(jnp.float32))
    yg = y.reshape(b, L, SSM_GROUPS, SSM_WIDTH // SSM_GROUPS)
    yg = yg * lax.rsqrt(jnp.mean(yg * yg, axis=-1, keepdims=True) + EPS)
    y = yg.reshape(b, L, SSM_WIDTH).astype(z.dtype) * g_ssm
    return y, h_f, h_b


def conv_ffn(h, w_up, conv_w, conv_b, w_down):
    u = dwconv_centred(h @ w_up, conv_w, conv_b)
    a, g = jnp.split(u, 2, axis=-1)
    return (a * jax.nn.silu(g)) @ w_down


def setup_inputs(seed: int = 0) -> dict:
    key = jax.random.key(seed)
    ks = jax.random.split(key, 24)
    nrm = lambda k, shape, s: jax.random.normal(k, shape, jnp.float32) * s
    dt0 = jnp.exp(jax.random.uniform(ks[14], (DEPTH, 2, SSM_HEADS), jnp.float32,
                                     minval=math.log(1e-3), maxval=math.log(1e-1)))
    return {
        'x': nrm(ks[0], (BATCH, SEQ, D_MODEL), 1.0),
        'c': nrm(ks[1], (BATCH, D_MODEL), 1.0),
        'ctx': nrm(ks[2], (BATCH, CTX_LEN, D_MODEL), 1.0),
        'c_ctx': nrm(ks[3], (D_MODEL,), 1.0),
        'w_mod': nrm(ks[4], (DEPTH, D_MODEL, 6 * D_MODEL), 0.5 * D_MODEL ** -0.5),
        'b_mod': nrm(ks[5], (DEPTH, 6 * D_MODEL), 0.01),
        'g_mix': 1.0 + nrm(ks[6], (DEPTH, D_MODEL), 0.01),
        'w_in': nrm(ks[7], (DEPTH, D_MODEL, IN_COLS), D_MODEL ** -0.5),
        'g_q': 1.0 + nrm(ks[8], (DEPTH, HEAD_DIM), 0.01),
        'g_k': 1.0 + nrm(ks[9], (DEPTH, HEAD_DIM), 0.01),
        'sink': nrm(ks[10], (DEPTH, ATTN_HEADS), 0.5),
        'ssm_conv_w': nrm(ks[11], (DEPTH, SSM_CONV, XBC_WIDTH), SSM_CONV ** -0.5),
        'ssm_conv_b': nrm(ks[12], (DEPTH, XBC_WIDTH), 0.01),
        'a_log': jnp.log(jax.random.uniform(ks[13], (DEPTH, 2, SSM_HEADS), jnp.float32, minval=1.0, maxval=16.0)),
        'dt_bias': dt0 + jnp.log(-jnp.expm1(-dt0)),
        'd_skip': 1.0 + nrm(ks[15], (DEPTH, SSM_HEADS), 0.01),
        'g_ssm': 1.0 + nrm(ks[16], (DEPTH, SSM_WIDTH), 0.01),
        'w_out': nrm(ks[17], (DEPTH, D_MODEL, D_MODEL), D_MODEL ** -0.5),
        'g_ffn': 1.0 + nrm(ks[18], (DEPTH, D_MODEL), 0.01),
        'w_up': nrm(ks[19], (DEPTH, D_MODEL, 2 * D_FF), D_MODEL ** -0.5),
        'ffn_conv_w': nrm(ks[20], (DEPTH, FFN_CONV, 2 * D_FF), FFN_CONV ** -0.5),
        'ffn_conv_b': nrm(ks[21], (DEPTH, 2 * D_FF), 0.01),
        'w_down': nrm(ks[22], (DEPTH, D_FF, D_MODEL), D_FF ** -0.5),
    }


def reference(x, c, ctx, c_ctx, w_mod, b_mod, g_mix, w_in, g_q, g_k, sink, ssm_conv_w, ssm_conv_b,
              a_log, dt_bias, d_skip, g_ssm, w_out, g_ffn, w_up, ffn_conv_w, ffn_conv_b, w_down):
    b, L = x.shape[0], x.shape[1]
    rope = axial_rope_tables(L)
    for layer in range(DEPTH):
        mod = jax.nn.silu(c) @ w_mod[layer] + b_mod[layer]
        mod_c = jax.nn.silu(c_ctx) @ w_mod[layer] + b_mod[layer]
        sh_a, sc_a, ga_a, sh_f, sc_f, ga_f = jnp.split(mod[:, None, :], 6, axis=-1)
        csh_a, csc_a, cga_a, csh_f, csc_f, cga_f = jnp.split(mod_c[None, None, :], 6, axis=-1)

        h = modulate(rmsnorm(x, g_mix[layer]), sh_a, sc_a)
        hc = modulate(rmsnorm(ctx, g_mix[layer]), csh_a, csc_a)
        q, k, v, z, xbc, dt_raw = split_projection(h @ w_in[layer])
        qc, kc, vc, zc, xbcc, dt_rawc = split_projection(hc @ w_in[layer])

        q = apply_axial_rope(rmsnorm(to_heads(q, ATTN_HEADS), g_q[layer]), rope)
        k = apply_axial_rope(rmsnorm(to_heads(k, KV_HEADS), g_k[layer]), rope)
        v = to_heads(v, KV_HEADS)
        kc = rmsnorm(to_heads(kc, KV_HEADS), g_k[layer])
        vc = to_heads(vc, KV_HEADS)
        attn = window_attention(q, k, v, kc, vc, sink[layer])

        ssm_params = (ssm_conv_w[layer], ssm_conv_b[layer], a_log[layer], dt_bias[layer], d_skip[layer], g_ssm[layer])
        h0 = jnp.zeros((b, SSM_HEADS, SSM_HEAD_DIM, D_STATE), jnp.float32)
        y_ssm_c, hf_c, hb_c = ssm_mixer(zc, xbcc, dt_rawc, *ssm_params, h0, h0)
        y_ssm, _, _ = ssm_mixer(z, xbc, dt_raw, *ssm_params, hf_c, hb_c)

        x_new = x + ga_a * (jnp.concatenate([attn, y_ssm], axis=-1) @ w_out[layer])
        x_new = x_new + ga_f * conv_ffn(modulate(rmsnorm(x_new, g_ffn[layer]), sh_f, sc_f),
                                        w_up[layer], ffn_conv_w[layer], ffn_conv_b[layer], w_down[layer])

        if layer < DEPTH - 1:
            qc = rmsnorm(to_heads(qc, ATTN_HEADS), g_q[layer])
            attn_c = context_attention(qc, kc, vc, sink[layer])
            ctx = ctx + cga_a * (jnp.concatenate([attn_c, y_ssm_c], axis=-1) @ w_out[layer])
            ctx = ctx + cga_f * conv_ffn(modulate(rmsnorm(ctx, g_ffn[layer]), csh_f, csc_f),
                                         w_up[layer], ffn_conv_w[layer], ffn_conv_b[layer], w_down[layer])
        x = x_new
    return x
```

```python
from contextlib import ExitStack
import numpy as np
import concourse.bass as bass
import concourse.mybir as mybir
from concourse.bass_utils import run_bass_kernel_spmd

F32 = mybir.dt.float32
BF16 = mybir.dt.bfloat16
ALU = mybir.AluOpType
AF = mybir.ActivationFunctionType
AX = mybir.AxisListType
ENGS = ("sp", "act", "pool", "dve", "pe")

L, LC, D = 2048, 256, 1024
NT, NCT = 16, 2
INC = 2320
DFF = 2816
EPS = 1e-6
NEG = -30000.0
SCHED = True
SLACK = 200.0
STRICT_SAME_ENGINE = True


class Op:
    __slots__ = ("eng", "fn", "reads", "writes", "dma", "idx", "seq", "waits", "inc", "key", "clock", "barrier", "rw", "cost", "lat", "glue", "tab")

    def __init__(self, eng, fn, reads, writes, dma, key):
        self.eng, self.fn, self.reads, self.writes, self.dma, self.key = eng, fn, reads, writes, dma, key
        self.waits = []
        self.inc = None
        self.seq = None
        self.barrier = False
        self.cost = 100.0
        self.lat = None
        self.glue = False
        self.tab = None


class Prog:
    def __init__(self, nc):
        self.nc = nc
        self.ops = []

    def op(self, eng, fn, reads=(), writes=(), dma=False, key=None, cost=100.0, glue=False, lat=None):
        reads, writes = tuple(reads), tuple(writes)
        rw = writes
        extra = tuple(r for r in reads if isinstance(r, str) and r[:2] in ("pb", "sc") and r not in writes)
        o = Op(eng, fn, reads, writes + extra, dma, key)
        o.rw = rw
        o.cost, o.glue, o.lat = cost, glue, lat
        o.idx = len(self.ops)
        self.ops.append(o)
        return o

    def dma(self, eng, out, in_, reads=(), writes=(), key=None, **kw):
        assert key is not None
        nbytes = 4
        for d in out.shape:
            nbytes *= d
        return self.op(eng, lambda e: e.dma_start(out=out, in_=in_, **kw), reads, writes, dma=True, key=key,
                       cost=(150.0 if eng == "sp" else 1200.0), lat=2000.0 + nbytes / 200.0)

    @staticmethod
    def _deps(ops):
        last_w, readers = {}, {}
        deps = []
        for i, o in enumerate(ops):
            d = set()
            for r in o.reads:
                if r in last_w:
                    d.add(last_w[r])
            for w in o.writes:
                if w in last_w:
                    d.add(last_w[w])
                d.update(readers.get(w, ()))
            for r in o.reads:
                readers.setdefault(r, []).append(i)
            for w in o.writes:
                last_w[w] = i
                readers[w] = []
            d.discard(i)
            deps.append(d)
        return deps

    def schedule(self):
        segs, cur = [], []
        for o in self.ops:
            if o.barrier:
                segs.append(cur)
                segs.append([o])
                cur = []
            else:
                cur.append(o)
        segs.append(cur)
        new_ops = []
        for si_, seg in enumerate(segs):
            tab_aware = (si_ == 0)
            if len(seg) <= 1:
                new_ops += seg
                continue
            deps = self._deps(seg)
            node_of = [0] * len(seg)
            nodes = []
            last_on_eng = {}
            for i, o in enumerate(seg):
                if o.glue and o.eng in last_on_eng:
                    n = node_of[last_on_eng[o.eng]]
                    nodes[n].append(i)
                else:
                    n = len(nodes)
                    nodes.append([i])
                node_of[i] = n
                last_on_eng[o.eng] = i
            ndeps = []
            for n, mem in enumerate(nodes):
                d = set()
                for i in mem:
                    for j in deps[i]:
                        if node_of[j] != n:
                            d.add(node_of[j])
                ndeps.append(d)
            users = [[] for _ in nodes]
            for n, d in enumerate(ndeps):
                for j in d:
                    users[j].append(n)
            nrem = [len(d) for d in ndeps]
            ncost = [sum(seg[i].cost for i in mem) for mem in nodes]
            blevel = [0.0] * len(nodes)
            for n in range(len(nodes) - 1, -1, -1):
                o0_ = seg[nodes[n][0]]
                c_ = o0_.lat if o0_.lat is not None else ncost[n] + 60.0
                blevel[n] = c_ + max([blevel[u] for u in users[n]] + [0.0])
            ready_t = [0.0] * len(nodes)
            finish = [0.0] * len(nodes)
            eng_free = {e: 0.0 for e in ENGS}
            ready = {e: [] for e in ENGS}
            for n in range(len(nodes)):
                if nrem[n] == 0:
                    ready[seg[nodes[n][0]].eng].append(n)
            order = []
            cur_tab = [None]
            nsched = 0
            while nsched < len(nodes):
                best = None
                for e in ENGS:
                    lst = ready[e]
                    if not lst:
                        continue
                    ef = eng_free[e]
                    bn, bs = None, None
                    pen = {}
                    if tab_aware and e == "act":
                        for n in lst:
                            tb = seg[nodes[n][0]].tab
                            if tb is not None and tb != cur_tab[0]:
                                pen[n] = 1300.0
                    smin = min((ready_t[n] if ready_t[n] > ef else ef) + pen.get(n, 0.0) for n in lst)
                    for n in lst:
                        st_ = (ready_t[n] if ready_t[n] > ef else ef) + pen.get(n, 0.0)
                        if st_ > smin + (120.0, 0.0, 200.0, 0.0, 200.0)[min(si_, 4)]:
                            continue
                        if bn is None or blevel[n] > blevel[bn] + 1e-9 or (abs(blevel[n] - blevel[bn]) <= 1e-9 and n < bn):
                            bn, bs = n, st_
                    if best is None or bs < best[0] - 1e-9 or (abs(bs - best[0]) <= 1e-9 and bn < best[1]):
                        best = (bs, bn, e)
                bs, bn, e = best
                ready[e].remove(bn)
                occ = sum(seg[i].cost for i in nodes[bn])
                if tab_aware and e == "act":
                    tb = seg[nodes[bn][0]].tab
                    if tb is not None:
                        if tb != cur_tab[0]:
                            occ += 1300.0
                            bs -= 1300.0
                        cur_tab[0] = tb
                o0 = seg[nodes[bn][0]]
                eng_free[e] = bs + occ
                finish[bn] = bs + (o0.lat if o0.lat is not None else occ + 60.0)
                order.append((bs, bn))
                nsched += 1
                for u in users[bn]:
                    nrem[u] -= 1
                    if finish[bn] > ready_t[u]:
                        ready_t[u] = finish[bn]
                    if nrem[u] == 0:
                        ready[seg[nodes[u][0]].eng].append(u)
            order.sort()
            for _, n in order:
                for i in nodes[n]:
                    new_ops.append(seg[i])
            self.est = getattr(self, "est", []) + [round(max(eng_free.values()) / 1e3)]
        for i, o in enumerate(new_ops):
            o.idx = i
        self.ops = new_ops

    def barrier(self):
        o = self.op("sp", lambda e: e.nop(), (), ())
        o.barrier = True
        o.rw = ()
        return o

    def finalize(self, stack):
        nc, ops = self.nc, self.ops
        last_w, readers = {}, {}
        deps = [None] * len(ops)
        bar = None
        for o in ops:
            d = set()
            if o.barrier:
                for k, v in last_w.items():
                    d.add(v)
                for k, v in readers.items():
                    d.update(v)
                for k in list(last_w.keys()) + list(readers.keys()):
                    last_w[k] = o.idx
                    readers[k] = []
                if bar is not None:
                    d.add(bar)
                bar = o.idx
            else:
                for r in o.reads:
                    if r in last_w:
                        d.add(last_w[r])
                    elif bar is not None:
                        d.add(bar)
                for w in o.writes:
                    if w in last_w:
                        d.add(last_w[w])
                    elif bar is not None:
                        d.add(bar)
                    for rd in readers.get(w, ()):
                        d.add(rd)
                for r in o.reads:
                    readers.setdefault(r, []).append(o.idx)
                for w in o.writes:
                    last_w[w] = o.idx
                    readers[w] = []
            d.discard(o.idx)
            deps[o.idx] = d

        def needs_sem(dop, o):
            if dop.dma:
                return True
            if dop.eng == o.eng and not o.dma and not o.barrier and not dop.barrier:
                if dop.eng == "pe":
                    return False
                if STRICT_SAME_ENGINE:
                    return True
                return bool(set(dop.rw) & set(o.reads))
            return True

        has_dep = [False] * len(ops)
        for o in ops:
            for di in deps[o.idx]:
                if needs_sem(ops[di], o):
                    has_dep[di] = True
        self.eng_sem = {e: stack.enter_context(nc.semaphore("s_" + e)) for e in ENGS}
        self.dma_sem = {}
        eng_cnt = {e: 0 for e in ENGS}
        dma_cnt = {}
        for o in ops:
            if o.dma:
                if o.key not in self.dma_sem:
                    self.dma_sem[o.key] = stack.enter_context(nc.semaphore("d_%d" % len(self.dma_sem)))
                dma_cnt[o.key] = dma_cnt.get(o.key, 0) + 1
                o.seq = dma_cnt[o.key]
                o.inc = (self.dma_sem[o.key], 16)
            elif has_dep[o.idx]:
                eng_cnt[o.eng] += 1
                o.seq = eng_cnt[o.eng]
                o.inc = (self.eng_sem[o.eng], 1)
        known = {e: {} for e in ENGS}
        dma_seen = {}
        for o in ops:
            kn = known[o.eng]
            need = {}
            for di in deps[o.idx]:
                dop = ops[di]
                if not needs_sem(dop, o):
                    continue
                if dop.dma:
                    sem = self.dma_sem[dop.key]
                    val = 16 * dma_seen.get(dop.key, 0)
                    k = ("d", dop.key)
                else:
                    sem = self.eng_sem[dop.eng]
                    val = dop.seq
                    k = ("e", dop.eng)
                if kn.get(k, 0) >= val:
                    continue
                if k not in need or need[k][1] < val:
                    need[k] = (sem, val, dop)
            for k, (sem, val, dop) in need.items():
                o.waits.append((sem, val))
                kn[k] = val
                if not dop.dma and dop.clock:
                    for k2, v2 in dop.clock.items():
                        if kn.get(k2, 0) < v2:
                            kn[k2] = v2
            if o.dma:
                dma_seen[o.key] = dma_seen.get(o.key, 0) + 1
                o.clock = None
            else:
                o.clock = dict(kn)
                if o.seq is not None:
                    o.clock[("e", o.eng)] = o.seq
        self.dma_total = dma_cnt

    def emit(self, block, final_dma_keys=()):
        by_eng = {e: [o for o in self.ops if o.eng == e] for e in ENGS}
        dma_sem, dma_total = self.dma_sem, self.dma_total

        def run(e, lst, final=False):
            for o in lst:
                for sem, val in o.waits:
                    e.wait_ge(sem, val)
                ins = o.fn(e)
                if o.inc is not None:
                    ins.then_inc(o.inc[0], o.inc[1])
            if final:
                for k in final_dma_keys:
                    e.wait_ge(dma_sem[k], 16 * dma_total[k])

        @block.sync
        def _(e):
            run(e, by_eng["sp"], final=True)

        @block.scalar
        def _(e):
            run(e, by_eng["act"])

        @block.gpsimd
        def _(e):
            run(e, by_eng["pool"])

        @block.vector
        def _(e):
            run(e, by_eng["dve"])

        @block.tensor
        def _(e):
            run(e, by_eng["pe"])


C_ID, C_BONES, C_RT, C_ONES, C_TRIF, C_TRIB, C_AMP, C_AMN = [i * 128 for i in range(8)]
C_SMF = 1024
C_SMB = 2048
NCST = 3072


def host_consts():
    c = np.zeros((128, NCST), np.float32)
    j = np.arange(128)[:, None]
    i = np.arange(128)[None, :]
    c[:, C_ID:C_ID + 128] = (j == i)
    c[:, C_BONES:C_BONES + 128] = (j // 64 == i // 64)
    rt = np.zeros((128, 128), np.float32)
    for m in range(128):
        if m % 32 < 16:
            rt[m + 16, m] = -1.0
        else:
            rt[m - 16, m] = 1.0
    c[:, C_RT:C_RT + 128] = rt
    c[:, C_ONES:C_ONES + 128] = 1.0
    c[:, C_TRIF:C_TRIF + 128] = (j <= i)
    c[:, C_TRIB:C_TRIB + 128] = (j >= i)
    c[:, C_AMP:C_AMP + 128] = (j >= i)
    c[:, C_AMN:C_AMN + 128] = (j <= i)
    mf = np.where(i >= j, 0.0, NEG).astype(np.float32)
    mb = np.where(i <= j, 0.0, NEG).astype(np.float32)
    c[:, C_SMF:C_SMF + 1024] = np.tile(mf, (1, 8))
    c[:, C_SMB:C_SMB + 1024] = np.tile(mb, (1, 8))
    return c


def host_rope():
    t = np.arange(L)
    quarter = 16
    freqs = 10000.0 ** (-np.arange(quarter, dtype=np.float64) / quarter)
    pos_r = (t // 64).astype(np.float64)
    pos_c = (t % 64).astype(np.float64)
    out = np.zeros((128, 2, L), np.float32)
    for p in range(128):
        d = p % 64
        f = freqs[d % 16]
        ang = (pos_r if d < 32 else pos_c) * f
        out[p, 0] = np.cos(ang)
        out[p, 1] = np.sin(ang)
    return out.reshape(128, 2 * L)


PC_CC, PC_GMIX, PC_GFFN, PC_GQ, PC_GK, PC_SCW, PC_SCB, PC_FCW, PC_FCB = 0, 16, 24, 32, 33, 34, 58, 66, 198
NPC = 242
PR_DTB, PR_ALOG, PR_DSK, PR_GSSM, PR_SINK = 0, 16, 32, 40, 552
NPR = 552 + 512


def host_params(c_b, c_ctx, g_mix, g_ffn, g_q, g_k, scw, scb, fcw, fcb, dt_bias, a_log, d_skip, g_ssm, sink):
    pc = np.zeros((128, NPC), np.float32)
    cc = np.stack([c_b.reshape(8, 128).T, c_ctx.reshape(8, 128).T], axis=-1)
    pc[:, PC_CC:PC_CC + 16] = cc.reshape(128, 16)
    pc[:, PC_GMIX:PC_GMIX + 8] = g_mix.reshape(8, 128).T
    pc[:, PC_GFFN:PC_GFFN + 8] = g_ffn.reshape(8, 128).T
    pc[:, PC_GQ] = np.tile(g_q, 2)
    pc[:, PC_GK] = np.tile(g_k, 2)
    pc[:, PC_SCW:PC_SCW + 24] = scw.reshape(3, 8, 128).transpose(2, 1, 0).reshape(128, 24)
    pc[:, PC_SCB:PC_SCB + 8] = scb.reshape(8, 128).T
    pc[:, PC_FCW:PC_FCW + 132] = fcw.reshape(3, 44, 128).transpose(2, 1, 0).reshape(128, 132)
    pc[:, PC_FCB:PC_FCB + 44] = fcb.reshape(44, 128).T
    pr = np.zeros((128, NPR), np.float32)
    pr[:, PR_DTB:PR_DTB + 16] = dt_bias.reshape(16)[None]
    pr[:, PR_ALOG:PR_ALOG + 16] = a_log.reshape(16)[None]
    pr[:, PR_DSK:PR_DSK + 8] = d_skip[None]
    pr[:, PR_GSSM:PR_GSSM + 512] = g_ssm[None]
    sl = np.zeros((128, 2, 2, 128), np.float32)
    for g in range(2):
        for pair in range(2):
            for half in range(2):
                sl[half * 64:(half + 1) * 64, g, pair, :] = sink[4 * g + 2 * pair + half]
    pr[:, PR_SINK:PR_SINK + 512] = sl.reshape(128, 512)
    return pc, pr


def build_nc(debug=(), stop_after=99, skip_att=False, skip_ssm=False):
    nc = bass.Bass("TRN2", target_bir_lowering=False)
    dt_ = nc.dram_tensor
    x_h = dt_("x", [L, D], F32, kind="ExternalInput").ap()
    ctx_h = dt_("ctx", [LC, D], F32, kind="ExternalInput").ap()
    wmod_h = dt_("w_mod", [6, 128, 8 * 1024], F32, kind="ExternalInput").ap()
    win_h = dt_("w_in", [128, 8 * INC], F32, kind="ExternalInput").ap()
    wout_h = dt_("w_out", [128, 8 * D], F32, kind="ExternalInput").ap()
    wup_h = dt_("w_up", [22, 128, 2 * 8 * 128], F32, kind="ExternalInput").ap()
    wdn_h = dt_("w_down", [128, 22 * D], F32, kind="ExternalInput").ap()
    pc_h = dt_("pcol", [128, NPC], F32, kind="ExternalInput").ap()
    pr_h = dt_("prow", [128, NPR], F32, kind="ExternalInput").ap()
    bm_h = dt_("bmod2", [2, 6 * D], F32, kind="ExternalInput").ap()
    cst_h = dt_("cst", [128, NCST], F32, kind="ExternalInput").ap()
    rope_h = dt_("rope", [128, 2 * L], F32, kind="ExternalInput").ap()
    out_h = dt_("out", [L, D], F32, kind="ExternalOutput").ap()
    dbg_h = {}
    for name, shape in debug:
        dbg_h[name] = dt_("dbg_" + name, list(shape), F32, kind="ExternalOutput").ap()

    st = ExitStack()
    P = Prog(nc)
    ARENA_WORDS = 52992
    arena = st.enter_context(nc.sbuf_tensor("arena", [128, ARENA_WORDS], F32))
    pbig = st.enter_context(nc.psum_tensor("pbig", [128, 4096], F32))
    pb = [pbig[:, i * 512:(i + 1) * 512] for i in range(8)]
    pbb = [p.bitcast(BF16) for p in pb]

    def V(off, nbytes, dtype=F32, shape=None):
        assert off % 4 == 0 and off + nbytes <= ARENA_WORDS * 4, (off, nbytes)
        v = arena[:, off // 4:(off + nbytes + 3) // 4]
        if dtype != F32:
            v = v.bitcast(dtype)
        return v

    class Alloc:
        def __init__(self, base, limit):
            self.o, self.limit = base, limit

        def get(self, cols, dtype=F32):
            nb = cols * (4 if dtype == F32 else 2)
            nb = (nb + 31) // 32 * 32
            v = V(self.o, nb, dtype)
            self.o += nb
            assert self.o <= self.limit, (self.o, self.limit)
            return v[:, 0:cols]

    KB = 1024
    A0 = Alloc(0, 24 * KB)
    cstb = A0.get(NCST, BF16)
    identf = A0.get(128)
    onesf = A0.get(128)
    pcol = A0.get(NPC)
    prow = A0.get(NPR)
    modcol = A0.get(96)
    derived = A0.get(64)
    ga_a_bc = A0.get(1024)
    ga_f_bc = A0.get(1024)
    arow = A0.get(16)
    esink = A0.get(512)
    ccs = A0.get(16, BF16)
    ident = cstb[:, C_ID:C_ID + 128]
    bones = cstb[:, C_BONES:C_BONES + 128]
    rtm = cstb[:, C_RT:C_RT + 128]
    onesb = cstb[:, C_ONES:C_ONES + 128]
    trif = cstb[:, C_TRIF:C_TRIF + 128]
    trib = cstb[:, C_TRIB:C_TRIB + 128]
    amp = cstb[:, C_AMP:C_AMP + 128]
    amn = cstb[:, C_AMN:C_AMN + 128]
    smf = cstb[:, C_SMF:C_SMF + 1024]
    smb = cstb[:, C_SMB:C_SMB + 1024]
    modc3 = modcol.rearrange("p (j r) -> p j r", r=2)

    def fsz(ap):
        n = 1
        for d in ap.shape[1:]:
            n *= d
        return n

    def mm(out, lhsT, rhs, start, stop, reads, writes, tp=None):
        c = max(fsz(out), 32) / 1.92 + 6.0
        if tp is None:
            return P.op("pe", lambda e: e.matmul(out, lhsT, rhs, start=start, stop=stop), reads, writes, cost=c, glue=not start)
        return P.op("pe", lambda e: e.matmul(out, lhsT, rhs, start=start, stop=stop, tile_position=tp), reads, writes, cost=c, glue=not start)

    def tr(out, in_, idn, reads, writes):
        return P.op("pe", lambda e: e.transpose(out, in_, idn), reads, writes, cost=140.0)

    def act(out, in_, func, reads, writes, bias=None, scale=None, accum=None):
        kw = {}
        if bias is not None:
            kw["bias"] = bias
        if scale is not None:
            kw["scale"] = scale
        if accum is not None:
            kw["accum_out"] = accum
        o = P.op("act", lambda e: e.activation(out, in_, func, **kw), reads, writes, cost=200.0 + fsz(out) / 1.2)
        o.tab = "silu" if func == AF.Silu else ("lnexp" if func in (AF.Exp, AF.Ln) else None)
        return o

    def tt(out, a, b, op, reads, writes, eng="dve"):
        return P.op(eng, lambda e: e.tensor_tensor(out, a, b, op), reads, writes, cost=(80.0 + fsz(out) * 1.05) * (1.0 if eng == "dve" else 2.2))

    def ts(out, a, s1, s2, op0, op1, reads, writes, eng="dve"):
        if s2 is None:
            return P.op(eng, lambda e: e.tensor_scalar(out, a, s1, None, op0), reads, writes, cost=(110.0 + fsz(out) * 0.68) * (1.0 if eng == "dve" else 3.0))
        return P.op(eng, lambda e: e.tensor_scalar(out, a, s1, s2, op0, op1), reads, writes, cost=(110.0 + fsz(out) * 0.68) * (1.0 if eng == "dve" else 3.0))

    def stt(out, a, s, b, op0, op1, reads, writes, eng="dve"):
        return P.op(eng, lambda e: e.scalar_tensor_tensor(out, a, s, b, op0, op1), reads, writes, cost=85.0 + fsz(out) * 1.3)

    def cp(out, in_, reads, writes, eng="dve"):
        return P.op(eng, lambda e: e.tensor_copy(out, in_), reads, writes, cost=70.0 + fsz(out) / 1.0)

    def bc_mid(ap2d, n):
        return ap2d.unsqueeze(1).to_broadcast([ap2d.shape[0], n, ap2d.shape[1]])

    def bc_last(ap2d, n):
        return ap2d.unsqueeze(2).to_broadcast([ap2d.shape[0], ap2d.shape[1], n])

    dbg_n = [0]

    def dump(name, ap, reads):
        if name in dbg_h:
            dbg_n[0] += 1
            P.dma("pool", dbg_h[name], ap, reads=reads, key="dbg_" + name)

    P.dma("pool", cstb, cst_h, writes=["cstb"], key="cstb")
    P.dma("sp", identf, cst_h[:, C_ID:C_ID + 128], writes=["identf"], key="identf")
    P.dma("sp", onesf, cst_h[:, C_ONES:C_ONES + 128], writes=["onesf"], key="onesf")
    P.dma("sp", pcol, pc_h, writes=["pcol"], key="pcol")
    P.dma("sp", prow, pr_h, writes=["prow"], key="prow")
    T0 = Alloc(54784, 54784 + 52 * KB)
    AL = ["sz", "xstok", "btok", "bT"]
    modrow = T0.get(2048)
    bmod = T0.get(2048)
    wmb = [T0.get(8 * 1024, BF16) for _ in range(2)]
    P.dma("sp", bmod[0:2, :], bm_h[:, 0:2048], reads=AL, writes=["bmod"], key="bmod")
    act(ccs, pcol[:, PC_CC:PC_CC + 16], AF.Silu, ["pcol"], ["ccs"])
    ccs3 = ccs.rearrange("p (k r) -> p k r", r=2)
    for blk in range(2):
        wb = wmb[blk % 2]
        wk = "wmb%d" % (blk % 2)
        P.dma("pool", wb, wmod_h[blk], reads=AL, writes=[wk], key=wk)
        wb3 = wb.rearrange("p (k n) -> p k n", k=8)
        for half in range(2):
            pk = "pb%d" % half
            for k in range(8):
                mm(pb[half][0:2, :], ccs3[:, k, :], wb3[:, k, half * 512:(half + 1) * 512], k == 0, k == 7, ["ccs", wk] + AL, [pk])
            c0 = blk * 1024 + half * 512
            tt(modrow[0:2, c0:c0 + 512], pb[half][0:2, :], bmod[0:2, c0:c0 + 512], ALU.add, [pk, "bmod"] + AL, ["modrow"])
    for j in range(16):
        tr(pb[2][:, 2 * j:2 * j + 2], modrow[0:2, j * 128:(j + 1) * 128], identf[0:2, 0:2], ["modrow", "identf"] + AL, ["pb2"])
    cp(modcol[:, 0:32], pb[2][:, 0:32], ["pb2"], ["modcol"])
    for r in range(2):
        stt(derived[:, 8 * r:8 * r + 8], modc3[:, 8:16, r], 1.0, pcol[:, PC_GMIX:PC_GMIX + 8], ALU.add, ALU.mult, ["modcol", "pcol"], ["derived"])
        cp(derived[:, 16 + 8 * r:24 + 8 * r], modc3[:, 0:8, r], ["modcol"], ["derived"])
    Sa = [derived[:, 0:8], derived[:, 8:16]]
    sha = [derived[:, 16:24], derived[:, 24:32]]
    Sf, shf = derived[:, 32:40], derived[:, 40:48]
    act(arow, prow[:, PR_ALOG:PR_ALOG + 16], AF.Exp, ["prow"], ["arow"])
    ts(arow, arow, -1.0, None, ALU.mult, None, ["arow"], ["arow"])
    act(esink, prow[:, PR_SINK:PR_SINK + 512], AF.Exp, ["prow"], ["esink"])
    dump("modcol", modcol, ["modcol"])

    def finish():
        if SCHED:
            P.schedule()
            print("sched est us per segment:", P.est, flush=True)
        P.finalize(st)
        keys = [k for k in P.dma_sem if isinstance(k, str) and (k.startswith("dbg_") or k.startswith("out"))]
        with nc.Block() as block:
            P.emit(block, final_dma_keys=keys)
        st.close()
        return nc

    if stop_after < 1:
        return finish()
    R1 = Alloc(24 * KB, 118 * KB)
    qT = R1.get(4 * L, BF16).rearrange("p (c t) -> p c t", c=4)
    kd = R1.get(2 * (L + LC), BF16).rearrange("p (g t) -> p g t", g=2)
    vtok = R1.get(18 * 128, BF16).rearrange("p (t c) -> p t c", t=18)
    sz = R1.get(16 * 512, BF16).rearrange("p (t c) -> p t c", t=16)
    xstok = R1.get(18 * 512, BF16).rearrange("p (t c) -> p t c", t=18)
    btok = R1.get(18 * 256, BF16).rearrange("p (t c) -> p t c", t=18)
    bT = R1.get(2 * (L + LC), BF16).rearrange("p (g t) -> p g t", g=2)
    cT = R1.get(2 * (L + LC), BF16).rearrange("p (g t) -> p g t", g=2)
    dtt = R1.get(18 * 16).rearrange("p (t c) -> p t c", t=18)
    dtab = R1.get(18 * 16, BF16).rearrange("p (t c) -> p t c", t=18)

    T1 = Alloc(118 * KB, 207 * KB)
    winb = T1.get(8 * INC, BF16).rearrange("p (k n) -> p k n", k=8)
    ropew = [T1.get(2 * 512, BF16).rearrange("p (s t) -> p s t", s=2) for _ in range(2)]
    hTw = [T1.get(8 * 514, BF16).rearrange("p (k n) -> p k n", k=8) for _ in range(2)]
    xt = [T1.get(1024) for _ in range(2)]
    xnb = [T1.get(1024, BF16)] * 2
    stat = T1.get(8)
    ustage = [T1.get(516) for _ in range(2)]
    tconv = [T1.get(512) for _ in range(2)]
    xbcs = T1.get(4 * 512, BF16).rearrange("p (c n) -> p c n", c=4)
    sqb = T1.get(512, BF16)
    qgb = T1.get(512, BF16)
    rstd = T1.get(512)
    t1 = T1.get(512)
    t2 = T1.get(512)
    dtmp = T1.get(16)
    for k in range(8):
        P.dma("pool", winb[:, k, :], win_h[:, k * INC:(k + 1) * INC], writes=["winb%d" % k], key="winb%d" % k)

    windows = [(False, 512 * w, 512) for w in range(4)] + [(True, 0, 256)]
    xtile_n = [0]
    for wi, (is_ctx, a0, nown) in enumerate(windows):
        hT = hTw[wi % 2]
        hk = "hT%d" % (wi % 2)
        src = ctx_h if is_ctx else x_h
        seqlen = LC if is_ctx else L
        r = 1 if is_ctx else 0
        rope, rk = ropew[wi % 2], "rope%d" % (wi % 2)
        if not is_ctx:
            P.dma("pool", rope, rope_h.rearrange("p (s t) -> p s t", s=2)[:, :, a0:a0 + 512], writes=[rk], key=rk)
        lo, hi = a0 - 1, a0 + nown + 1
        if lo < 0:
            P.op("pool", lambda e, hT=hT: e.memset(hT[:, :, 0:1], 0.0), [], [hk])
        if hi > seqlen:
            P.op("pool", lambda e, hT=hT, c=nown + 1: e.memset(hT[:, :, c:c + 1], 0.0), [], [hk])
        row = max(lo, 0)
        rend = min(hi, seqlen)
        while row < rend:
            n = min(128, rend - row)
            col0 = row - lo
            i = xtile_n[0] % 2
            xtile_n[0] += 1
            xk, nk = "xt%d" % i, "xnb0"
            P.dma("sp", xt[i][0:n, :], src[row:row + n, :], writes=[xk], key=xk)
            act(xnb[i][0:n, :], xt[i][0:n, :], AF.Square, [xk], [nk, "stat"], accum=stat[0:n, 0:1])
            act(stat[0:n, 1:2], stat[0:n, 0:1], AF.Ln, ["stat"], ["stat"], bias=EPS, scale=1.0 / D)
            act(stat[0:n, 2:3], stat[0:n, 1:2], AF.Exp, ["stat"], ["stat"], scale=-0.5)
            ts(xnb[i][0:n, :], xt[i][0:n, :], stat[0:n, 2:3], None, ALU.mult, None, [xk, "stat"], [nk])
            for k in range(8):
                tr(pbb[0][:, k * 128:k * 128 + n], xnb[i][0:n, k * 128:(k + 1) * 128], ident[0:n, 0:n], [nk, "cstb"], ["pb0"])
            src3 = pbb[0][:, 0:1024].rearrange("p (k n) -> p k n", k=8)[:, :, 0:n]
            dst3 = hT[:, :, col0:col0 + n]
            tt(dst3, src3, bc_last(Sa[r], n), ALU.mult, ["pb0", "derived"], [hk])
            tt(dst3, dst3, bc_last(sha[r], n), ALU.add, [hk, "derived"], [hk])
            row += n
        if wi == 0:
            dump("hT0", hT.rearrange("p k n -> p (k n)"), [hk])
        tok0 = (L if is_ctx else 0) + a0
        chunks = ([] if is_ctx else [("q", c) for c in range(4)]) + [("k", g) for g in range(2)]
        for ci, (kind, c) in enumerate(chunks):
            pbk = "pb%d" % (1 + ci % 2)
            pt = pb[1 + ci % 2]
            if kind == "q":
                for k in range(8):
                    mm(pt[:, 0:nown], winb[:, k, c * 128:(c + 1) * 128], hT[:, k, 1:1 + nown], k == 0, k == 7, ["winb%d" % k, hk], [pbk])
                gcol = pcol[:, PC_GQ:PC_GQ + 1]
            else:
                for half in range(2):
                    for k in range(8):
                        mm(pt[half * 64:(half + 1) * 64, 0:nown], winb[:, k, 512 + c * 64:512 + (c + 1) * 64], hT[:, k, 1:1 + nown],
                           k == 0, k == 7, ["winb%d" % k, hk], [pbk], tp=(0, 64 * half))
                gcol = pcol[:, PC_GK:PC_GK + 1]
            act(sqb[:, 0:nown], pt[:, 0:nown], AF.Square, [pbk], ["sqb"])
            act(qgb[:, 0:nown], pt[:, 0:nown], AF.Copy, [pbk, "pcol"], ["qgb"], scale=gcol)
            mm(pb[3][:, 0:nown], bones, sqb[:, 0:nown], True, True, ["cstb", "sqb"], ["pb3"])
            act(rstd[:, 0:nown], pb[3][:, 0:nown], AF.Ln, ["pb3"], ["rstd"], bias=EPS, scale=1.0 / 64)
            act(rstd[:, 0:nown], rstd[:, 0:nown], AF.Exp, ["rstd"], ["rstd"], scale=-0.5)
            dst = qT[:, c, a0:a0 + nown] if kind == "q" else kd[:, c, tok0:tok0 + nown]
            dk = "qT" if kind == "q" else "kd"
            if is_ctx:
                tt(dst, qgb[:, 0:nown], rstd[:, 0:nown], ALU.mult, ["qgb", "rstd"], [dk])
            else:
                mm(pb[4][:, 0:nown], rtm, qgb[:, 0:nown], True, True, ["cstb", "qgb"], ["pb4"])
                tt(t1[:, 0:nown], qgb[:, 0:nown], rope[:, 0, 0:nown], ALU.mult, ["qgb", rk], ["t1"])
                tt(t2[:, 0:nown], pb[4][:, 0:nown], rope[:, 1, 0:nown], ALU.mult, ["pb4", rk], ["t2"])
                tt(t1[:, 0:nown], t1[:, 0:nown], t2[:, 0:nown], ALU.add, ["t1", "t2"], ["t1"])
                tt(dst, t1[:, 0:nown], rstd[:, 0:nown], ALU.mult, ["t1", "rstd"], [dk])
        ncol = nown + 2
        for c in range(8):
            pbk = "pb%d" % (1 + c % 2)
            pt = pb[1 + c % 2]
            us, uk = ustage[c % 2], "us%d" % (c % 2)
            tc_, tk = tconv[c % 2], "tc%d" % (c % 2)
            w0c = 1280 + c * 128
            nmain = min(512, ncol)
            for k in range(8):
                mm(pt[:, 0:nmain], winb[:, k, w0c:w0c + 128], hT[:, k, 0:nmain], k == 0, k == 7, ["winb%d" % k, hk], [pbk])
            act(us[:, 0:nmain], pt[:, 0:nmain], AF.Copy, [pbk], [uk])
            if ncol > 512:
                for k in range(8):
                    mm(pb[7][:, 0:ncol - 512], winb[:, k, w0c:w0c + 128], hT[:, k, 512:ncol], k == 0, k == 7, ["winb%d" % k, hk], ["pb7"])
                act(us[:, 512:ncol], pb[7][:, 0:ncol - 512], AF.Copy, ["pb7"], [uk])
            wv = pcol[:, PC_SCW + 3 * c:PC_SCW + 3 * c + 3]
            bv = pcol[:, PC_SCB + c:PC_SCB + c + 1]
            ts(tc_[:, 0:nown], us[:, 1:1 + nown], wv[:, 1:2], bv, ALU.mult, ALU.add, [uk, "pcol"], [tk])
            stt(tc_[:, 0:nown], us[:, 0:nown], wv[:, 0:1], tc_[:, 0:nown], ALU.mult, ALU.add, [uk, tk, "pcol"], [tk])
            stt(tc_[:, 0:nown], us[:, 2:2 + nown], wv[:, 2:3], tc_[:, 0:nown], ALU.mult, ALU.add, [uk, tk, "pcol"], [tk])
            if c < 4:
                act(xbcs[:, c, 0:nown], tc_[:, 0:nown], AF.Silu, [tk], ["xbcs%d" % c])
            elif c < 6:
                act(bT[:, c - 4, tok0:tok0 + nown], tc_[:, 0:nown], AF.Silu, [tk], ["bT"])
            else:
                act(cT[:, c - 6, tok0:tok0 + nown], tc_[:, 0:nown], AF.Silu, [tk], ["cT"])
        for j in range(nown // 128):
            tile = tok0 // 128 + j
            cs = slice(j * 128, (j + 1) * 128)
            for c in range(4):
                tr(pbb[5][:, c * 128:(c + 1) * 128], xbcs[:, c, cs], ident, ["xbcs%d" % c, "cstb"], ["pb5"])
            for g in range(2):
                tr(pbb[5][:, 512 + g * 128:512 + (g + 1) * 128], bT[:, g, tok0 + j * 128:tok0 + (j + 1) * 128], ident, ["bT", "cstb"], ["pb5"])
            cp(xstok[:, tile, :], pbb[5][:, 0:512], ["pb5"], ["xstok"])
            cp(btok[:, tile, :], pbb[5][:, 512:768], ["pb5"], ["btok"])
            hs = hT[:, :, 1 + j * 128:1 + (j + 1) * 128]
            for k in range(8):
                mm(pb[6][:, 0:128], hs[:, k, :], winb[:, k, 640:768], k == 0, k == 7, [hk, "winb%d" % k], ["pb6"])
            for k in range(8):
                mm(pb[6][:, 128:144], hs[:, k, :], winb[:, k, 2304:2320], k == 0, k == 7, [hk, "winb%d" % k], ["pb6"])
            act(vtok[:, tile, :], pb[6][:, 0:128], AF.Copy, ["pb6"], ["vtok"])
            tt(dtmp, pb[6][:, 128:144], prow[:, PR_DTB:PR_DTB + 16], ALU.add, ["pb6", "prow"], ["dtmp"])
            act(dtmp, dtmp, AF.Exp, ["dtmp"], ["dtmp"])
            act(dtt[:, tile, :], dtmp, AF.Ln, ["dtmp"], ["dtt"], bias=1.0)
            tt(dtab[:, tile, :], dtt[:, tile, :], arow, ALU.mult, ["dtt", "arow"], ["dtab"])
            if not is_ctx:
                for k in range(8):
                    mm(pb[4][:, :], hs[:, k, :], winb[:, k, 768:1280], k == 0, k == 7, [hk, "winb%d" % k], ["pb4"])
                act(sz[:, tile, :], pb[4][:, :], AF.Silu, ["pb4"], ["sz"])
    dump("qT", qT.rearrange("p c t -> p (c t)"), ["qT"])
    dump("kd", kd.rearrange("p g t -> p (g t)"), ["kd"])
    dump("vtok", vtok.rearrange("p t c -> p (t c)"), ["vtok"])
    dump("sz", sz.rearrange("p t c -> p (t c)"), ["sz"])
    dump("xstok", xstok.rearrange("p t c -> p (t c)"), ["xstok"])
    dump("btok", btok.rearrange("p t c -> p (t c)"), ["btok"])
    dump("cT", cT.rearrange("p g t -> p (g t)"), ["cT"])
    dump("dtt", dtt.rearrange("p t c -> p (t c)"), ["dtt"])

    if stop_after < 2:
        return finish()
    P.barrier()
    T2 = Alloc(118 * KB, 207 * KB)
    ybw = T2.get(16 * 512, BF16).rearrange("p (t c) -> p t c", t=16)
    yT = T2.get(8 * L, BF16).rearrange("p (c t) -> p c t", c=8)

    T3 = Alloc(T2.o, 207 * KB)
    pT = [T3.get(5 * 512, BF16).rearrange("p (b n) -> p b n", b=5) for _ in range(2)]
    rsum = T3.get(256)
    rhs1 = [T3.get(1024, BF16) for _ in range(2)]
    nacol = [T3.get(8) for _ in range(2)]
    eacol = [T3.get(8) for _ in range(2)]
    dend = [T3.get(8) for _ in range(2)]
    cdec = [T3.get(8) for _ in range(2)]
    lt = [T3.get(1024, BF16) for _ in range(2)]
    xdt = [T3.get(512, BF16) for _ in range(2)]
    xw = [T3.get(512, BF16) for _ in range(2)]
    mt = T3.get(1024, BF16)
    yoffs = T3.get(512)
    ytmp = T3.get(512)
    gst = T3.get(8)
    ynb = T3.get(512, BF16)
    state = T3.get(512)
    stateb = T3.get(512, BF16)
    wmp = T3.get(8 * 256, BF16).rearrange("p (k n) -> p k n", k=8)
    bmp = T3.get(256)
    mrp = T3.get(256)

    def mod_piece(q):
        blk, off = q // 4, (q % 4) * 256
        P.dma("pool", wmp, wmod_h[blk].rearrange("p (k n) -> p k n", k=8)[:, :, off:off + 256], writes=["wmp"], key="wmp")
        P.dma("sp", bmp[0:2, :], bm_h[:, q * 256:(q + 1) * 256], writes=["bmp"], key="bmp")
        for k in range(8):
            mm(pb[6][0:2, 0:256], ccs3[:, k, :], wmp[:, k, :], k == 0, k == 7, ["ccs", "wmp"], ["pb6"])
        tt(mrp[0:2, :], pb[6][0:2, 0:256], bmp[0:2, :], ALU.add, ["pb6", "bmp"], ["mrp"])
        for j in range(2):
            tr(pb[6][:, 2 * j:2 * j + 2], mrp[0:2, j * 128:(j + 1) * 128], identf[0:2, 0:2], ["mrp", "identf"], ["pb6"])
        cp(modcol[:, q * 4:q * 4 + 4], pb[6][:, 0:4], ["pb6"], ["modcol"])
        if blk in (2, 5):
            dst = ga_a_bc if blk == 2 else ga_f_bc
            mm(pb[6][:, 0:256], onesf[0:1, 0:128], mrp[0:1, :], True, True, ["onesf", "mrp"], ["pb6"])
            cp(dst[:, off:off + 256], pb[6][:, 0:256], ["pb6"], ["gabc"])

    def att_unit(qt, g):
        qs = slice(qt * 128, (qt + 1) * 128)
        idx = (qt * 2 + g) % 2
        pTt, pk = pT[idx], "pT%d" % idx
        blocks = []
        if qt > 0:
            blocks.append((qt - 1, amp))
        blocks.append((qt, None))
        if qt < NT - 1:
            blocks.append((qt + 1, amn))
        blocks += [(16, None), (17, None)]
        for bi, (kt, mask) in enumerate(blocks):
            for r4 in range(4):
                par = r4 % 2
                pair = r4 // 2
                c = 2 * g + pair
                hp = slice(64 * par, 64 * par + 64)
                mm(pb[par][:, pair * 128:(pair + 1) * 128], kd[hp, g, kt * 128:(kt + 1) * 128], qT[hp, c, qs],
                   True, True, ["kd", "qT"], ["sc0"], tp=(64 * par, 0))
            src = pbig[:, 0:1024].rearrange("p (b n) -> p b n", b=2)[:, :, 0:256]
            act(pTt[:, bi, :].rearrange("p (b n) -> p b n", b=2), src, AF.Exp, ["sc0"], [pk], scale=0.125)
            if mask is not None:
                pv = pTt[:, bi, :].rearrange("p (q i) -> p q i", q=4)
                tt(pv, pv, bc_mid(mask, 4), ALU.mult, [pk, "cstb"], [pk], eng="pool")
        nb = len(blocks)
        for which in range(2):
            for bi, (kt, mask) in enumerate(blocks):
                for par in range(2):
                    op_ = slice(64 * par, 64 * par + 64)
                    lhs = vtok[:, kt, g * 64:(g + 1) * 64] if which == 0 else onesb[:, 0:64]
                    P.op("pe", lambda e, o_=pb[2][op_, which * 256:(which + 1) * 256], l_=lhs, r_=pTt[:, bi, par * 256:(par + 1) * 256],
                         s_=(bi == 0), t_=(bi == nb - 1), p_=par: e.matmul(o_, l_, r_, start=s_, stop=t_, tile_position=(0, 64 * p_)),
                         ["vtok", "cstb", pk], ["pb2"], cost=256 / 1.92 / 2 + 6, glue=not (bi == 0 and par == 0))
        for pair in range(2):
            act(rsum[:, pair * 128:(pair + 1) * 128], pb[2][:, 256 + pair * 128:256 + (pair + 1) * 128], AF.Ln, ["pb2", "esink"], ["rsum"],
                bias=esink[:, g * 256 + pair * 128:g * 256 + pair * 128 + 1])
        act(rsum, rsum, AF.Exp, ["rsum"], ["rsum"], scale=-1.0)
        tt(yT[:, 2 * g:2 * g + 2, qs], pb[2][:, 0:256].rearrange("p (a i) -> p a i", a=2),
           rsum.rearrange("p (a i) -> p a i", a=2), ALU.mult, ["pb2", "rsum"], ["yT"])

    ssm_n = [0]

    def ssm_unit(direction, tile):
        pp = ssm_n[0] % 2
        ssm_n[0] += 1
        sfx = "_%d" % pp
        rhs1_, nacol_, eacol_, dend_, cdec_, lt_, xdt_, xw_ = rhs1[pp], nacol[pp], eacol[pp], dend[pp], cdec[pp], lt[pp], xdt[pp], xw[pp]
        tri = trib if direction else trif
        smask = smb if direction else smf
        dsl = slice(8 * direction, 8 * direction + 8)
        iend = 0 if direction else 127
        is_ctx = tile >= 16
        tsl = slice(tile * 128, (tile + 1) * 128)
        CB, X = pb[3], pb[4]
        tt(rhs1_.rearrange("p (h i) -> p h i", h=8), bc_mid(tri, 8), bc_last(dtab[:, tile, dsl], 128), ALU.mult,
           ["cstb", "dtab"], ["rhs1" + sfx], eng="pool")
        mm(pb[5][:, 0:8], tri, dtab[:, tile, dsl], True, True, ["cstb", "dtab"], ["pb5"])
        ts(nacol_, pb[5][:, 0:8], -1.0, None, ALU.mult, None, ["pb5"], ["nacol" + sfx])
        if not is_ctx:
            act(eacol_, pb[5][:, 0:8], AF.Exp, ["pb5"], ["eacol" + sfx])
        for half in range(2):
            hs_ = slice(half * 512, (half + 1) * 512)
            mm(CB, onesb, rhs1_[:, hs_], True, False, ["cstb", "rhs1" + sfx], ["pb3"])
            mm(CB, ident, smask[:, hs_], False, True, ["cstb"], ["pb3"])
            totv = CB.rearrange("p (h i) -> p h i", h=4)[:, :, iend]
            tt(dend_[:, half * 4:half * 4 + 4], totv, nacol_[:, half * 4:half * 4 + 4], ALU.add, ["pb3", "nacol" + sfx], ["dend" + sfx])
            act(cdec_[:, half * 4:half * 4 + 4], totv, AF.Exp, ["pb3"], ["cdec" + sfx])
            if not is_ctx:
                for hh in range(4):
                    h = half * 4 + hh
                    act(lt_[:, h * 128:(h + 1) * 128], CB[:, hh * 128:(hh + 1) * 128], AF.Exp,
                        ["pb3", "nacol" + sfx], ["lt" + sfx], bias=nacol_[:, h:h + 1])
        act(dend_, dend_, AF.Exp, ["dend" + sfx], ["dend" + sfx])
        tt(xdt_.rearrange("p (h c) -> p h c", h=8), xstok[:, tile, :].rearrange("p (h c) -> p h c", h=8),
           bc_last(dtt[:, tile, dsl], 64), ALU.mult, ["xstok", "dtt"], ["xdt" + sfx], eng="pool")
        tt(xw_.rearrange("p (h c) -> p h c", h=8), xdt_.rearrange("p (h c) -> p h c", h=8), bc_last(dend_, 64), ALU.mult,
           ["xdt" + sfx, "dend" + sfx], ["xw" + sfx], eng="pool")
        if not is_ctx:
            for g in range(2):
                mm(pb[6][:, g * 128:(g + 1) * 128], bT[:, g, tsl], cT[:, g, tsl], True, True, ["bT", "cT"], ["pb6"])
            tt(mt.rearrange("p (g r i) -> p g r i", g=2, r=4), lt_.rearrange("p (g r i) -> p g r i", g=2, r=4),
               pb[6][:, 0:256].rearrange("p (g i) -> p g i", g=2).unsqueeze(2).to_broadcast([128, 2, 4, 128]),
               ALU.mult, ["lt" + sfx, "pb6"], ["mt"])
            for g in range(2):
                mm(X[:, g * 256:(g + 1) * 256], cT[:, g, tsl], stateb[:, g * 256:(g + 1) * 256], True, True,
                   ["cT", "stateb"], ["pb4"])
            tt(yoffs.rearrange("p (h c) -> p h c", h=8), X.rearrange("p (h c) -> p h c", h=8), bc_last(eacol_, 64),
               ALU.mult, ["pb4", "eacol" + sfx], ["yoffs"])
            for h in range(8):
                mm(X[:, h * 64:(h + 1) * 64], mt[:, h * 128:(h + 1) * 128], xdt_[:, h * 64:(h + 1) * 64], True, True,
                   ["mt", "xdt" + sfx], ["pb4"])
        for g in range(2):
            mm(pb[7][:, g * 256:(g + 1) * 256], btok[:, tile, g * 128:(g + 1) * 128], xw_[:, g * 256:(g + 1) * 256], True, True,
               ["btok", "xw" + sfx], ["pb7"])
        if not is_ctx:
            if direction:
                tt(ytmp, X, yoffs, ALU.add, ["pb4", "yoffs"], ["ytmp"])
                tt(yoffs.rearrange("p (h c) -> p h c", h=8), xstok[:, tile, :].rearrange("p (h c) -> p h c", h=8),
                   bc_last(prow[:, PR_DSK:PR_DSK + 8], 64), ALU.mult, ["xstok", "prow"], ["yoffs"], eng="pool")
                tt(ybw[:, tile, :], ytmp, yoffs, ALU.add, ["ytmp", "yoffs"], ["ybw"])
            else:
                tt(ytmp, X, yoffs, ALU.add, ["pb4", "yoffs"], ["ytmp"])
        tt(state.rearrange("p (h c) -> p h c", h=8), state.rearrange("p (h c) -> p h c", h=8), bc_last(cdec_, 64), ALU.mult,
           ["state", "cdec" + sfx], ["state"])
        tt(state, state, pb[7], ALU.add, ["state", "pb7"], ["state"])
        act(stateb, state, AF.Copy, ["state"], ["stateb"])
        if not is_ctx and not direction:
            tt(ytmp, ytmp, ybw[:, tile, :], ALU.add, ["ytmp", "ybw"], ["ytmp"])
            tt(ytmp, ytmp, sz[:, tile, :], ALU.mult, ["ytmp", "sz"], ["ytmp"])
            for g in range(2):
                act(mt[:, g * 256:(g + 1) * 256], ytmp[:, g * 256:(g + 1) * 256], AF.Square, ["ytmp"], ["mt", "gst"],
                    accum=gst[:, g:g + 1])
            act(gst[:, 2:4], gst[:, 0:2], AF.Ln, ["gst"], ["gst"], bias=EPS, scale=1.0 / 256)
            act(gst[:, 4:6], gst[:, 2:4], AF.Exp, ["gst"], ["gst"], scale=-0.5)
            tt(ytmp.rearrange("p (g c) -> p g c", g=2), ytmp.rearrange("p (g c) -> p g c", g=2), bc_last(gst[:, 4:6], 256),
               ALU.mult, ["ytmp", "gst"], ["ytmp"])
            tt(ynb, ytmp, prow[:, PR_GSSM:PR_GSSM + 512], ALU.mult, ["ytmp", "prow"], ["ynb"], eng="pool")
            for c in range(4):
                tr(pbb[7][:, c * 128:(c + 1) * 128], ynb[:, c * 128:(c + 1) * 128], ident, ["ynb", "cstb"], ["pb7"])
            cp(yT[:, 4:8, tsl], pbb[7][:, 0:512].rearrange("p (c i) -> p c i", c=4), ["pb7"], ["yT"])

    ssm_units = []
    for direction in (1, 0):
        ctx_tiles = [17, 16] if direction else [16, 17]
        lat_tiles = list(range(15, -1, -1)) if direction else list(range(16))
        ssm_units.append(("reset", direction))
        for tile in ctx_tiles + lat_tiles:
            ssm_units.append((direction, tile))
    att_units = [(qt, g) for qt in range(NT) for g in range(2)]
    ai = 0
    pieces = list(range(8, 24))
    nunit = [0]
    for u in ssm_units:
        nunit[0] += 1
        if pieces and u[0] != "reset":
            mod_piece(pieces.pop(0))
        if u[0] == "reset":
            P.op("pool", lambda e: e.memset(state, 0.0), [], ["state"])
            P.op("pool", lambda e: e.memset(stateb, 0.0), [], ["stateb"])
            continue
        if ai < len(att_units):
            if not skip_att:
                att_unit(*att_units[ai])
            ai += 1
        if not skip_ssm:
            ssm_unit(*u)
    while ai < len(att_units):
        if not skip_att:
            att_unit(*att_units[ai])
        ai += 1
    assert not pieces
    stt(derived[:, 32:40], modc3[:, 32:40, 0], 1.0, pcol[:, PC_GFFN:PC_GFFN + 8], ALU.add, ALU.mult, ["modcol", "pcol"], ["derived2"])
    cp(derived[:, 40:48], modc3[:, 24:32, 0], ["modcol"], ["derived2"])
    dump("yTa", yT[:, 0:4, :].rearrange("p c t -> p (c t)"), ["yT"])
    dump("yTs", yT[:, 4:8, :].rearrange("p c t -> p (c t)"), ["yT"])

    if stop_after < 4:
        return finish()
    P.barrier()
    xnew = V(24 * KB, 64 * KB).rearrange("p (t c) -> p t c", t=16)
    T5 = Alloc(88 * KB, 118 * KB)
    woutb = T5.get(8 * 1024, BF16).rearrange("p (k n) -> p k n", k=8)
    xt4 = [T5.get(1024) for _ in range(2)]
    P.dma("pool", woutb.rearrange("p k n -> p (k n)"), wout_h, writes=["woutb"], key="woutb")
    for t in range(NT):
        i = t % 2
        xk = "x4_%d" % i
        P.dma("sp", xt4[i], x_h[t * 128:(t + 1) * 128, :], writes=[xk], key=xk)
        for half in range(2):
            pt, pk = pb[(2 * t + half) % 4], "pb%d" % ((2 * t + half) % 4)
            for k in range(8):
                mm(pt[:, :], yT[:, k, t * 128:(t + 1) * 128], woutb[:, k, half * 512:(half + 1) * 512], k == 0, k == 7,
                   ["yT", "woutb"], [pk])
            hsl = slice(half * 512, (half + 1) * 512)
            tt(xnew[:, t, hsl], pt[:, :], ga_a_bc[:, hsl], ALU.mult, [pk, "gabc"], ["xnew%d" % t])
            tt(xnew[:, t, hsl], xnew[:, t, hsl], xt4[i][:, hsl], ALU.add, ["xnew%d" % t, xk], ["xnew%d" % t])
    dump("xnew", xnew.rearrange("p t c -> p (t c)"), ["xnew%d" % t for t in range(NT)])

    if stop_after < 5:
        return finish()
    S6 = Alloc(166 * KB, 207 * KB)
    U6 = Alloc(88 * KB, 166 * KB)
    h2w = [S6.get(8 * 514, BF16).rearrange("p (k n) -> p k n", k=8) for _ in range(2)]
    wupb = [S6.get(2 * 8 * 128, BF16).rearrange("p (s k n) -> p s k n", s=2, k=8) for _ in range(3)]
    sg5 = S6.get(512)
    xn5 = [S6.get(1024, BF16)] * 2
    stat5 = S6.get(8)
    ost = [S6.get(512) for _ in range(2)]
    wdnb = U6.get(22 * 1024, BF16).rearrange("p (k n) -> p k n", k=22)
    actT = U6.get(22 * 512, BF16).rearrange("p (k n) -> p k n", k=22)
    tc5 = [U6.get(512) for _ in range(6)] + [S6.get(512) for _ in range(2)]
    P.op("sp", lambda e: e.nop(), [], ["woutb", "yT", "x4_0", "x4_1", "actT"] + ["wdnb%d" % k for k in range(22)]
         + ["tc5_%d" % i for i in range(6)], cost=50.0)
    for k in range(22):
        P.dma("pool", wdnb[:, k, :], wdn_h[:, k * D:(k + 1) * D], writes=["wdnb%d" % k], key="wdnb%d" % k)
    ost_n = [0]

    def issue_wup(n_):
        j_ = n_ % 22
        wi_ = n_ % 3
        wk_ = "wup%d" % wi_
        P.dma("pool", wupb[wi_].rearrange("p s k n -> p (s k n)"), wup_h[j_], writes=[wk_ + "_0", wk_ + "_1"], key=wk_)

    issue_wup(0)
    issue_wup(1)
    for w in range(4):
        a0 = 512 * w
        h2 = h2w[w % 2]
        hk = "h2_%d" % (w % 2)
        lo, hi = a0 - 1, a0 + 513
        if lo < 0:
            P.op("pool", lambda e, h2=h2: e.memset(h2[:, :, 0:1], 0.0), [], [hk])
        if hi > L:
            P.op("pool", lambda e, h2=h2: e.memset(h2[:, :, 513:514], 0.0), [], [hk])
        first_t = 0 if w == 0 else 4 * w + 1
        last_t = min(4 * w + 4, NT - 1)
        for t in range(first_t, last_t + 1):
            wt, jt_ = t // 4, t % 4
            hb_ = h2w[wt % 2]
            hkk = "h2_%d" % (wt % 2)
            nk = "xn5_0"
            xk = "xnew%d" % t
            act(xn5[0], xnew[:, t, :], AF.Square, [xk], [nk, "stat5"], accum=stat5[:, 0:1])
            act(stat5[:, 1:2], stat5[:, 0:1], AF.Ln, ["stat5"], ["stat5"], bias=EPS, scale=1.0 / D)
            act(stat5[:, 2:3], stat5[:, 1:2], AF.Exp, ["stat5"], ["stat5"], scale=-0.5)
            ts(xn5[0], xnew[:, t, :], stat5[:, 2:3], None, ALU.mult, None, [xk, "stat5"], [nk])
            for k in range(8):
                tr(pbb[0][:, k * 128:(k + 1) * 128], xn5[0][:, k * 128:(k + 1) * 128], ident, [nk, "cstb"], ["pb0"])
            src_all = pbb[0][:, 0:1024].rearrange("p (k n) -> p k n", k=8)
            targets = [(hb_, hkk, 0, 128, 1 + 128 * jt_)]
            if jt_ == 0 and wt > 0:
                targets.append((h2w[(wt - 1) % 2], "h2_%d" % ((wt - 1) % 2), 0, 1, 513))
            if jt_ == 3 and wt < 3:
                targets.append((h2w[(wt + 1) % 2], "h2_%d" % ((wt + 1) % 2), 127, 128, 0))
            for (hbuf, hkey, r0, r1, col0) in targets:
                n = r1 - r0
                src3 = src_all[:, :, r0:r1]
                dst3 = hbuf[:, :, col0:col0 + n]
                tt(dst3, src3, bc_last(Sf, n), ALU.mult, ["pb0", "derived2"], [hkey])
                tt(dst3, dst3, bc_last(shf, n), ALU.add, [hkey, "derived2"], [hkey])
        for j in range(22):
            n_ = w * 22 + j
            if n_ + 2 < 88:
                issue_wup(n_ + 2)
            wi = n_ % 3
            wb = wupb[wi]
            tks = []
            for s in range(2):
                wk = "wup%d_%d" % (wi, s)
                bi = (2 * j + s) % 4
                pt, pk = pb[1 + bi], "pb%d" % (1 + bi)
                ti = (2 * n_ + s) % len(tc5)
                tc_, tk = tc5[ti], "tc5_%d" % ti
                tks.append((tc_, tk))
                hbi = (7, 5, 6)[(2 * n_ + s) % 3]
                hbk = "pb%d" % hbi
                hal = pb[hbi]
                for k in range(8):
                    mm(pt[:, :], wb[:, s, k, :], h2[:, k, 0:512], k == 0, k == 7, [wk, hk], [pk])
                for k in range(8):
                    mm(hal[:, 0:2], wb[:, s, k, :], h2[:, k, 512:514], k == 0, k == 7, [wk, hk], [hbk])
                ch = s * 22 + j
                wv = pcol[:, PC_FCW + 3 * ch:PC_FCW + 3 * ch + 3]
                bv = pcol[:, PC_FCB + ch:PC_FCB + ch + 1]
                act(tc_[:, 0:511], pt[:, 1:512], AF.Identity, [pk, "pcol"], [tk], bias=bv, scale=wv[:, 1:2])
                act(tc_[:, 511:512], hal[:, 0:1], AF.Identity, [hbk, "pcol"], [tk], bias=bv, scale=wv[:, 1:2])
                stt(tc_, pt[:, 0:512], wv[:, 0:1], tc_, ALU.mult, ALU.add, [pk, tk, "pcol"], [tk])
                stt(tc_[:, 0:510], pt[:, 2:512], wv[:, 2:3], tc_[:, 0:510], ALU.mult, ALU.add, [pk, tk, "pcol"], [tk])
                stt(tc_[:, 510:512], hal[:, 0:2], wv[:, 2:3], tc_[:, 510:512], ALU.mult, ALU.add, [hbk, tk, "pcol"], [tk])
            (tca, tka), (tcg, tkg) = tks
            act(sg5, tcg, AF.Silu, [tkg], ["sg5"])
            tt(actT[:, j, :], tca, sg5, ALU.mult, [tka, "sg5"], ["actT"], eng="pool")
        for jt in range(4):
            t = a0 // 128 + jt
            for half in range(2):
                oi = ost_n[0] % 2
                ost_n[0] += 1
                ok = "ost%d" % oi
                pt, pk = pb[5 + half], "pb%d" % (5 + half)
                for k in range(22):
                    mm(pt[:, :], actT[:, k, jt * 128:(jt + 1) * 128], wdnb[:, k, half * 512:(half + 1) * 512], k == 0, k == 21,
                       ["actT", "wdnb%d" % k], [pk])
                hsl = slice(half * 512, (half + 1) * 512)
                tt(ost[oi], pt[:, :], ga_f_bc[:, hsl], ALU.mult, [pk, "gabc"], [ok])
                tt(ost[oi], ost[oi], xnew[:, t, hsl], ALU.add, [ok, "xnew%d" % t], [ok])
                P.dma("sp", out_h[t * 128:(t + 1) * 128, hsl], ost[oi], reads=[ok], key="out%d" % oi)

    return finish()


_NC_CACHE = {}


def make_in_maps(x, c, ctx, c_ctx, w_mod, b_mod, g_mix, w_in, g_q, g_k, sink, ssm_conv_w, ssm_conv_b,
                 a_log, dt_bias, d_skip, g_ssm, w_out, g_ffn, w_up, ffn_conv_w, ffn_conv_b, w_down):
    f = lambda a: np.ascontiguousarray(np.asarray(a, dtype=np.float32))
    cst = host_consts()
    rope = host_rope()
    bm2 = np.ascontiguousarray(np.tile(f(b_mod)[0][None, :], (2, 1)))
    c_ = np.ascontiguousarray
    wm = c_(f(w_mod)[0].reshape(8, 128, 6, 1024).transpose(2, 1, 0, 3).reshape(6, 128, 8 * 1024))
    wi = c_(f(w_in)[0].reshape(8, 128, INC).transpose(1, 0, 2).reshape(128, 8 * INC))
    wo = c_(f(w_out)[0].reshape(8, 128, D).transpose(1, 0, 2).reshape(128, 8 * D))
    wu = c_(f(w_up)[0].reshape(8, 128, 2, 22, 128).transpose(3, 1, 2, 0, 4).reshape(22, 128, 2 * 8 * 128))
    wd = c_(f(w_down)[0].reshape(22, 128, D).transpose(1, 0, 2).reshape(128, 22 * D))
    shared = {
        "w_mod": wm, "w_in": wi, "w_out": wo, "w_up": wu, "w_down": wd,
        "bmod2": bm2, "cst": cst, "rope": rope,
    }
    maps = []
    for b in range(8):
        pc, pr = host_params(f(c)[b], f(c_ctx), f(g_mix)[0], f(g_ffn)[0], f(g_q)[0], f(g_k)[0], f(ssm_conv_w)[0],
                             f(ssm_conv_b)[0], f(ffn_conv_w)[0], f(ffn_conv_b)[0], f(dt_bias)[0], f(a_log)[0],
                             f(d_skip)[0], f(g_ssm)[0], f(sink)[0])
        m = dict(shared)
        m.update({"x": f(x)[b], "ctx": f(ctx)[b], "pcol": pc, "prow": pr})
        maps.append(m)
    return maps


def kernel(**inputs):
    if "nc" not in _NC_CACHE:
        _NC_CACHE["nc"] = build_nc()
    nc = _NC_CACHE["nc"]
    maps = make_in_maps(**inputs)
    res = run_bass_kernel_spmd(nc, maps, core_ids=list(range(8)))
    return np.stack([np.asarray(r["out"], dtype=np.float32) for r in res.results], axis=0)
```

```python
from contextlib import ExitStack
import numpy as np
import concourse.bass as bass
import concourse.mybir as mybir
from concourse.bass_utils import run_bass_kernel_spmd

F32 = mybir.dt.float32
BF16 = mybir.dt.bfloat16
ALU = mybir.AluOpType
AF = mybir.ActivationFunctionType
AX = mybir.AxisListType
ENGS = ("sp", "act", "pool", "dve", "pe")

L, LC, D = 2048, 256, 1024
NT, NCT = 16, 2
INC = 2320
DFF = 2816
EPS = 1e-6
NEG = -30000.0
SCHED = True
SLACK = 200.0
STRICT_SAME_ENGINE = True


class Op:
    __slots__ = ("eng", "fn", "reads", "writes", "dma", "idx", "seq", "waits", "inc", "key", "clock", "barrier", "rw", "cost", "lat", "glue", "tab")

    def __init__(self, eng, fn, reads, writes, dma, key):
        self.eng, self.fn, self.reads, self.writes, self.dma, self.key = eng, fn, reads, writes, dma, key
        self.waits = []
        self.inc = None
        self.seq = None
        self.barrier = False
        self.cost = 100.0
        self.lat = None
        self.glue = False
        self.tab = None


class Prog:
    def __init__(self, nc):
        self.nc = nc
        self.ops = []

    def op(self, eng, fn, reads=(), writes=(), dma=False, key=None, cost=100.0, glue=False, lat=None):
        reads, writes = tuple(reads), tuple(writes)
        rw = writes
        extra = tuple(r for r in reads if isinstance(r, str) and r[:2] in ("pb", "sc") and r not in writes)
        o = Op(eng, fn, reads, writes + extra, dma, key)
        o.rw = rw
        o.cost, o.glue, o.lat = cost, glue, lat
        o.idx = len(self.ops)
        self.ops.append(o)
        return o

    def dma(self, eng, out, in_, reads=(), writes=(), key=None, **kw):
        assert key is not None
        nbytes = 4
        for d in out.shape:
            nbytes *= d
        return self.op(eng, lambda e: e.dma_start(out=out, in_=in_, **kw), reads, writes, dma=True, key=key,
                       cost=(150.0 if eng == "sp" else 1200.0), lat=2000.0 + nbytes / 200.0)

    @staticmethod
    def _deps(ops):
        last_w, readers = {}, {}
        deps = []
        for i, o in enumerate(ops):
            d = set()
            for r in o.reads:
                if r in last_w:
                    d.add(last_w[r])
            for w in o.writes:
                if w in last_w:
                    d.add(last_w[w])
                d.update(readers.get(w, ()))
            for r in o.reads:
                readers.setdefault(r, []).append(i)
            for w in o.writes:
                last_w[w] = i
                readers[w] = []
            d.discard(i)
            deps.append(d)
        return deps

    def schedule(self):
        segs, cur = [], []
        for o in self.ops:
            if o.barrier:
                segs.append(cur)
                segs.append([o])
                cur = []
            else:
                cur.append(o)
        segs.append(cur)
        new_ops = []
        for si_, seg in enumerate(segs):
            tab_aware = (si_ == 0)
            if len(seg) <= 1:
                new_ops += seg
                continue
            deps = self._deps(seg)
            node_of = [0] * len(seg)
            nodes = []
            last_on_eng = {}
            for i, o in enumerate(seg):
                if o.glue and o.eng in last_on_eng:
                    n = node_of[last_on_eng[o.eng]]
                    nodes[n].append(i)
                else:
                    n = len(nodes)
                    nodes.append([i])
                node_of[i] = n
                last_on_eng[o.eng] = i
            ndeps = []
            for n, mem in enumerate(nodes):
                d = set()
                for i in mem:
                    for j in deps[i]:
                        if node_of[j] != n:
                            d.add(node_of[j])
                ndeps.append(d)
            users = [[] for _ in nodes]
            for n, d in enumerate(ndeps):
                for j in d:
                    users[j].append(n)
            nrem = [len(d) for d in ndeps]
            ncost = [sum(seg[i].cost for i in mem) for mem in nodes]
            blevel = [0.0] * len(nodes)
            for n in range(len(nodes) - 1, -1, -1):
                o0_ = seg[nodes[n][0]]
                c_ = o0_.lat if o0_.lat is not None else ncost[n] + 60.0
                blevel[n] = c_ + max([blevel[u] for u in users[n]] + [0.0])
            ready_t = [0.0] * len(nodes)
            finish = [0.0] * len(nodes)
            eng_free = {e: 0.0 for e in ENGS}
            ready = {e: [] for e in ENGS}
            for n in range(len(nodes)):
                if nrem[n] == 0:
                    ready[seg[nodes[n][0]].eng].append(n)
            order = []
            cur_tab = [None]
            nsched = 0
            while nsched < len(nodes):
                best = None
                for e in ENGS:
                    lst = ready[e]
                    if not lst:
                        continue
                    ef = eng_free[e]
                    bn, bs = None, None
                    pen = {}
                    if tab_aware and e == "act":
                        for n in lst:
                            tb = seg[nodes[n][0]].tab
                            if tb is not None and tb != cur_tab[0]:
                                pen[n] = 1300.0
                    smin = min((ready_t[n] if ready_t[n] > ef else ef) + pen.get(n, 0.0) for n in lst)
                    for n in lst:
                        st_ = (ready_t[n] if ready_t[n] > ef else ef) + pen.get(n, 0.0)
                        if st_ > smin + (200.0, 0.0, 200.0, 0.0, 240.0)[min(si_, 4)]:
                            continue
                        if bn is None or blevel[n] > blevel[bn] + 1e-9 or (abs(blevel[n] - blevel[bn]) <= 1e-9 and n < bn):
                            bn, bs = n, st_
                    if best is None or bs < best[0] - 1e-9 or (abs(bs - best[0]) <= 1e-9 and bn < best[1]):
                        best = (bs, bn, e)
                bs, bn, e = best
                ready[e].remove(bn)
                occ = sum(seg[i].cost for i in nodes[bn])
                if tab_aware and e == "act":
                    tb = seg[nodes[bn][0]].tab
                    if tb is not None:
                        if tb != cur_tab[0]:
                            occ += 1300.0
                            bs -= 1300.0
                        cur_tab[0] = tb
                o0 = seg[nodes[bn][0]]
                eng_free[e] = bs + occ
                finish[bn] = bs + (o0.lat if o0.lat is not None else occ + 60.0)
                order.append((bs, bn))
                nsched += 1
                for u in users[bn]:
                    nrem[u] -= 1
                    if finish[bn] > ready_t[u]:
                        ready_t[u] = finish[bn]
                    if nrem[u] == 0:
                        ready[seg[nodes[u][0]].eng].append(u)
            order.sort()
            for _, n in order:
                for i in nodes[n]:
                    new_ops.append(seg[i])
            self.est = getattr(self, "est", []) + [round(max(eng_free.values()) / 1e3)]
        for i, o in enumerate(new_ops):
            o.idx = i
        self.ops = new_ops

    def barrier(self):
        o = self.op("sp", lambda e: e.nop(), (), ())
        o.barrier = True
        o.rw = ()
        return o

    def finalize(self, stack):
        nc, ops = self.nc, self.ops
        last_w, readers = {}, {}
        deps = [None] * len(ops)
        bar = None
        for o in ops:
            d = set()
            if o.barrier:
                for k, v in last_w.items():
                    d.add(v)
                for k, v in readers.items():
                    d.update(v)
                for k in list(last_w.keys()) + list(readers.keys()):
                    last_w[k] = o.idx
                    readers[k] = []
                if bar is not None:
                    d.add(bar)
                bar = o.idx
            else:
                for r in o.reads:
                    if r in last_w:
                        d.add(last_w[r])
                    elif bar is not None:
                        d.add(bar)
                for w in o.writes:
                    if w in last_w:
                        d.add(last_w[w])
                    elif bar is not None:
                        d.add(bar)
                    for rd in readers.get(w, ()):
                        d.add(rd)
                for r in o.reads:
                    readers.setdefault(r, []).append(o.idx)
                for w in o.writes:
                    last_w[w] = o.idx
                    readers[w] = []
            d.discard(o.idx)
            deps[o.idx] = d

        def needs_sem(dop, o):
            if dop.dma:
                return True
            if dop.eng == o.eng and not o.dma and not o.barrier and not dop.barrier:
                if dop.eng == "pe":
                    return False
                if STRICT_SAME_ENGINE:
                    return True
                return bool(set(dop.rw) & set(o.reads))
            return True

        has_dep = [False] * len(ops)
        for o in ops:
            for di in deps[o.idx]:
                if needs_sem(ops[di], o):
                    has_dep[di] = True
        self.eng_sem = {e: stack.enter_context(nc.semaphore("s_" + e)) for e in ENGS}
        self.dma_sem = {}
        eng_cnt = {e: 0 for e in ENGS}
        dma_cnt = {}
        for o in ops:
            if o.dma:
                if o.key not in self.dma_sem:
                    self.dma_sem[o.key] = stack.enter_context(nc.semaphore("d_%d" % len(self.dma_sem)))
                dma_cnt[o.key] = dma_cnt.get(o.key, 0) + 1
                o.seq = dma_cnt[o.key]
                o.inc = (self.dma_sem[o.key], 16)
            elif has_dep[o.idx]:
                eng_cnt[o.eng] += 1
                o.seq = eng_cnt[o.eng]
                o.inc = (self.eng_sem[o.eng], 1)
        known = {e: {} for e in ENGS}
        dma_seen = {}
        for o in ops:
            kn = known[o.eng]
            need = {}
            for di in deps[o.idx]:
                dop = ops[di]
                if not needs_sem(dop, o):
                    continue
                if dop.dma:
                    sem = self.dma_sem[dop.key]
                    val = 16 * dma_seen.get(dop.key, 0)
                    k = ("d", dop.key)
                else:
                    sem = self.eng_sem[dop.eng]
                    val = dop.seq
                    k = ("e", dop.eng)
                if kn.get(k, 0) >= val:
                    continue
                if k not in need or need[k][1] < val:
                    need[k] = (sem, val, dop)
            for k, (sem, val, dop) in need.items():
                o.waits.append((sem, val))
                kn[k] = val
                if not dop.dma and dop.clock:
                    for k2, v2 in dop.clock.items():
                        if kn.get(k2, 0) < v2:
                            kn[k2] = v2
            if o.dma:
                dma_seen[o.key] = dma_seen.get(o.key, 0) + 1
                o.clock = None
            else:
                o.clock = dict(kn)
                if o.seq is not None:
                    o.clock[("e", o.eng)] = o.seq
        self.dma_total = dma_cnt

    def emit(self, block, final_dma_keys=()):
        by_eng = {e: [o for o in self.ops if o.eng == e] for e in ENGS}
        dma_sem, dma_total = self.dma_sem, self.dma_total

        def run(e, lst, final=False):
            for o in lst:
                for sem, val in o.waits:
                    e.wait_ge(sem, val)
                ins = o.fn(e)
                if o.inc is not None:
                    ins.then_inc(o.inc[0], o.inc[1])
            if final:
                for k in final_dma_keys:
                    e.wait_ge(dma_sem[k], 16 * dma_total[k])

        @block.sync
        def _(e):
            run(e, by_eng["sp"], final=True)

        @block.scalar
        def _(e):
            run(e, by_eng["act"])

        @block.gpsimd
        def _(e):
            run(e, by_eng["pool"])

        @block.vector
        def _(e):
            run(e, by_eng["dve"])

        @block.tensor
        def _(e):
            run(e, by_eng["pe"])


C_ID, C_BONES, C_RT, C_ONES, C_TRIF, C_TRIB, C_AMP, C_AMN = [i * 128 for i in range(8)]
C_SMF = 1024
C_SMB = 2048
NCST = 3072


def host_consts():
    c = np.zeros((128, NCST), np.float32)
    j = np.arange(128)[:, None]
    i = np.arange(128)[None, :]
    c[:, C_ID:C_ID + 128] = (j == i)
    c[:, C_BONES:C_BONES + 128] = (j // 64 == i // 64)
    rt = np.zeros((128, 128), np.float32)
    for m in range(128):
        if m % 32 < 16:
            rt[m + 16, m] = -1.0
        else:
            rt[m - 16, m] = 1.0
    c[:, C_RT:C_RT + 128] = rt
    c[:, C_ONES:C_ONES + 128] = 1.0
    c[:, C_TRIF:C_TRIF + 128] = (j <= i)
    c[:, C_TRIB:C_TRIB + 128] = (j >= i)
    c[:, C_AMP:C_AMP + 128] = (j >= i)
    c[:, C_AMN:C_AMN + 128] = (j <= i)
    mf = np.where(i >= j, 0.0, NEG).astype(np.float32)
    mb = np.where(i <= j, 0.0, NEG).astype(np.float32)
    c[:, C_SMF:C_SMF + 1024] = np.tile(mf, (1, 8))
    c[:, C_SMB:C_SMB + 1024] = np.tile(mb, (1, 8))
    return c


def host_rope():
    t = np.arange(L)
    quarter = 16
    freqs = 10000.0 ** (-np.arange(quarter, dtype=np.float64) / quarter)
    pos_r = (t // 64).astype(np.float64)
    pos_c = (t % 64).astype(np.float64)
    out = np.zeros((128, 2, L), np.float32)
    for p in range(128):
        d = p % 64
        f = freqs[d % 16]
        ang = (pos_r if d < 32 else pos_c) * f
        out[p, 0] = np.cos(ang)
        out[p, 1] = np.sin(ang)
    return out.reshape(128, 2 * L)


PC_CC, PC_GMIX, PC_GFFN, PC_GQ, PC_GK, PC_SCW, PC_SCB, PC_FCW, PC_FCB = 0, 16, 24, 32, 33, 34, 58, 66, 198
NPC = 242
PR_DTB, PR_ALOG, PR_DSK, PR_GSSM, PR_SINK = 0, 16, 32, 40, 552
NPR = 552 + 512


def host_params(c_b, c_ctx, g_mix, g_ffn, g_q, g_k, scw, scb, fcw, fcb, dt_bias, a_log, d_skip, g_ssm, sink):
    pc = np.zeros((128, NPC), np.float32)
    cc = np.stack([c_b.reshape(8, 128).T, c_ctx.reshape(8, 128).T], axis=-1)
    pc[:, PC_CC:PC_CC + 16] = cc.reshape(128, 16)
    pc[:, PC_GMIX:PC_GMIX + 8] = g_mix.reshape(8, 128).T
    pc[:, PC_GFFN:PC_GFFN + 8] = g_ffn.reshape(8, 128).T
    pc[:, PC_GQ] = np.tile(g_q, 2)
    pc[:, PC_GK] = np.tile(g_k, 2)
    pc[:, PC_SCW:PC_SCW + 24] = scw.reshape(3, 8, 128).transpose(2, 1, 0).reshape(128, 24)
    pc[:, PC_SCB:PC_SCB + 8] = scb.reshape(8, 128).T
    pc[:, PC_FCW:PC_FCW + 132] = fcw.reshape(3, 44, 128).transpose(2, 1, 0).reshape(128, 132)
    pc[:, PC_FCB:PC_FCB + 44] = fcb.reshape(44, 128).T
    pr = np.zeros((128, NPR), np.float32)
    pr[:, PR_DTB:PR_DTB + 16] = dt_bias.reshape(16)[None]
    pr[:, PR_ALOG:PR_ALOG + 16] = a_log.reshape(16)[None]
    pr[:, PR_DSK:PR_DSK + 8] = d_skip[None]
    pr[:, PR_GSSM:PR_GSSM + 512] = g_ssm[None]
    sl = np.zeros((128, 2, 2, 128), np.float32)
    for g in range(2):
        for pair in range(2):
            for half in range(2):
                sl[half * 64:(half + 1) * 64, g, pair, :] = sink[4 * g + 2 * pair + half]
    pr[:, PR_SINK:PR_SINK + 512] = sl.reshape(128, 512)
    return pc, pr


def build_nc(debug=(), stop_after=99, skip_att=False, skip_ssm=False):
    nc = bass.Bass("TRN2", target_bir_lowering=False)
    dt_ = nc.dram_tensor
    x_h = dt_("x", [L, D], F32, kind="ExternalInput").ap()
    ctx_h = dt_("ctx", [LC, D], F32, kind="ExternalInput").ap()
    wmod_h = dt_("w_mod", [6, 128, 8 * 1024], F32, kind="ExternalInput").ap()
    win_h = dt_("w_in", [128, 8 * INC], F32, kind="ExternalInput").ap()
    wout_h = dt_("w_out", [128, 8 * D], F32, kind="ExternalInput").ap()
    wup_h = dt_("w_up", [22, 128, 2 * 8 * 128], F32, kind="ExternalInput").ap()
    wdn_h = dt_("w_down", [128, 22 * D], F32, kind="ExternalInput").ap()
    pc_h = dt_("pcol", [128, NPC], F32, kind="ExternalInput").ap()
    pr_h = dt_("prow", [128, NPR], F32, kind="ExternalInput").ap()
    bm_h = dt_("bmod2", [2, 6 * D], F32, kind="ExternalInput").ap()
    cst_h = dt_("cst", [128, NCST], F32, kind="ExternalInput").ap()
    rope_h = dt_("rope", [128, 2 * L], F32, kind="ExternalInput").ap()
    out_h = dt_("out", [L, D], F32, kind="ExternalOutput").ap()
    dbg_h = {}
    for name, shape in debug:
        dbg_h[name] = dt_("dbg_" + name, list(shape), F32, kind="ExternalOutput").ap()

    st = ExitStack()
    P = Prog(nc)
    ARENA_WORDS = 52992
    arena = st.enter_context(nc.sbuf_tensor("arena", [128, ARENA_WORDS], F32))
    pbig = st.enter_context(nc.psum_tensor("pbig", [128, 4096], F32))
    pb = [pbig[:, i * 512:(i + 1) * 512] for i in range(8)]
    pbb = [p.bitcast(BF16) for p in pb]

    def V(off, nbytes, dtype=F32, shape=None):
        assert off % 4 == 0 and off + nbytes <= ARENA_WORDS * 4, (off, nbytes)
        v = arena[:, off // 4:(off + nbytes + 3) // 4]
        if dtype != F32:
            v = v.bitcast(dtype)
        return v

    class Alloc:
        def __init__(self, base, limit):
            self.o, self.limit = base, limit

        def get(self, cols, dtype=F32):
            nb = cols * (4 if dtype == F32 else 2)
            nb = (nb + 31) // 32 * 32
            v = V(self.o, nb, dtype)
            self.o += nb
            assert self.o <= self.limit, (self.o, self.limit)
            return v[:, 0:cols]

    KB = 1024
    A0 = Alloc(0, 24 * KB)
    cstb = A0.get(NCST, BF16)
    identf = A0.get(128)
    onesf = A0.get(128)
    pcol = A0.get(NPC)
    prow = A0.get(NPR)
    modcol = A0.get(96)
    derived = A0.get(64)
    ga_a_bc = A0.get(1024)
    ga_f_bc = A0.get(1024)
    arow = A0.get(16)
    esink = A0.get(512)
    ccs = A0.get(16, BF16)
    ident = cstb[:, C_ID:C_ID + 128]
    bones = cstb[:, C_BONES:C_BONES + 128]
    rtm = cstb[:, C_RT:C_RT + 128]
    onesb = cstb[:, C_ONES:C_ONES + 128]
    trif = cstb[:, C_TRIF:C_TRIF + 128]
    trib = cstb[:, C_TRIB:C_TRIB + 128]
    amp = cstb[:, C_AMP:C_AMP + 128]
    amn = cstb[:, C_AMN:C_AMN + 128]
    smf = cstb[:, C_SMF:C_SMF + 1024]
    smb = cstb[:, C_SMB:C_SMB + 1024]
    modc3 = modcol.rearrange("p (j r) -> p j r", r=2)

    def fsz(ap):
        n = 1
        for d in ap.shape[1:]:
            n *= d
        return n

    def mm(out, lhsT, rhs, start, stop, reads, writes, tp=None):
        c = max(fsz(out), 32) / 1.92 + 6.0
        if tp is None:
            return P.op("pe", lambda e: e.matmul(out, lhsT, rhs, start=start, stop=stop), reads, writes, cost=c, glue=not start)
        return P.op("pe", lambda e: e.matmul(out, lhsT, rhs, start=start, stop=stop, tile_position=tp), reads, writes, cost=c, glue=not start)

    def tr(out, in_, idn, reads, writes):
        return P.op("pe", lambda e: e.transpose(out, in_, idn), reads, writes, cost=140.0)

    def act(out, in_, func, reads, writes, bias=None, scale=None, accum=None):
        kw = {}
        if bias is not None:
            kw["bias"] = bias
        if scale is not None:
            kw["scale"] = scale
        if accum is not None:
            kw["accum_out"] = accum
        o = P.op("act", lambda e: e.activation(out, in_, func, **kw), reads, writes, cost=200.0 + fsz(out) / 1.2)
        o.tab = "silu" if func == AF.Silu else ("lnexp" if func in (AF.Exp, AF.Ln) else None)
        return o

    def tt(out, a, b, op, reads, writes, eng="dve"):
        return P.op(eng, lambda e: e.tensor_tensor(out, a, b, op), reads, writes, cost=(80.0 + fsz(out) * 1.05) * (1.0 if eng == "dve" else 2.2))

    def ts(out, a, s1, s2, op0, op1, reads, writes, eng="dve"):
        if s2 is None:
            return P.op(eng, lambda e: e.tensor_scalar(out, a, s1, None, op0), reads, writes, cost=(110.0 + fsz(out) * 0.68) * (1.0 if eng == "dve" else 3.0))
        return P.op(eng, lambda e: e.tensor_scalar(out, a, s1, s2, op0, op1), reads, writes, cost=(110.0 + fsz(out) * 0.68) * (1.0 if eng == "dve" else 3.0))

    def stt(out, a, s, b, op0, op1, reads, writes, eng="dve"):
        return P.op(eng, lambda e: e.scalar_tensor_tensor(out, a, s, b, op0, op1), reads, writes, cost=85.0 + fsz(out) * 1.3)

    def cp(out, in_, reads, writes, eng="dve"):
        return P.op(eng, lambda e: e.tensor_copy(out, in_), reads, writes, cost=70.0 + fsz(out) / 1.0)

    def bc_mid(ap2d, n):
        return ap2d.unsqueeze(1).to_broadcast([ap2d.shape[0], n, ap2d.shape[1]])

    def bc_last(ap2d, n):
        return ap2d.unsqueeze(2).to_broadcast([ap2d.shape[0], ap2d.shape[1], n])

    dbg_n = [0]

    def dump(name, ap, reads):
        if name in dbg_h:
            dbg_n[0] += 1
            P.dma("pool", dbg_h[name], ap, reads=reads, key="dbg_" + name)

    P.dma("pool", cstb, cst_h, writes=["cstb"], key="cstb")
    P.dma("sp", identf, cst_h[:, C_ID:C_ID + 128], writes=["identf"], key="identf")
    P.dma("sp", onesf, cst_h[:, C_ONES:C_ONES + 128], writes=["onesf"], key="onesf")
    P.dma("sp", pcol, pc_h, writes=["pcol"], key="pcol")
    P.dma("sp", prow, pr_h, writes=["prow"], key="prow")
    T0 = Alloc(54784, 54784 + 52 * KB)
    AL = ["sz", "xstok", "btok", "bT"]
    modrow = T0.get(2048)
    bmod = T0.get(2048)
    wmb = [T0.get(8 * 1024, BF16) for _ in range(2)]
    P.dma("sp", bmod[0:2, :], bm_h[:, 0:2048], reads=AL, writes=["bmod"], key="bmod")
    act(ccs, pcol[:, PC_CC:PC_CC + 16], AF.Silu, ["pcol"], ["ccs"])
    ccs3 = ccs.rearrange("p (k r) -> p k r", r=2)
    for blk in range(2):
        wb = wmb[blk % 2]
        wk = "wmb%d" % (blk % 2)
        P.dma("pool", wb, wmod_h[blk], reads=AL, writes=[wk], key=wk)
        wb3 = wb.rearrange("p (k n) -> p k n", k=8)
        for half in range(2):
            pk = "pb%d" % half
            for k in range(8):
                mm(pb[half][0:2, :], ccs3[:, k, :], wb3[:, k, half * 512:(half + 1) * 512], k == 0, k == 7, ["ccs", wk] + AL, [pk])
            c0 = blk * 1024 + half * 512
            tt(modrow[0:2, c0:c0 + 512], pb[half][0:2, :], bmod[0:2, c0:c0 + 512], ALU.add, [pk, "bmod"] + AL, ["modrow"])
    for j in range(16):
        tr(pb[2][:, 2 * j:2 * j + 2], modrow[0:2, j * 128:(j + 1) * 128], identf[0:2, 0:2], ["modrow", "identf"] + AL, ["pb2"])
    cp(modcol[:, 0:32], pb[2][:, 0:32], ["pb2"], ["modcol"])
    for r in range(2):
        stt(derived[:, 8 * r:8 * r + 8], modc3[:, 8:16, r], 1.0, pcol[:, PC_GMIX:PC_GMIX + 8], ALU.add, ALU.mult, ["modcol", "pcol"], ["derived"])
        cp(derived[:, 16 + 8 * r:24 + 8 * r], modc3[:, 0:8, r], ["modcol"], ["derived"])
    Sa = [derived[:, 0:8], derived[:, 8:16]]
    sha = [derived[:, 16:24], derived[:, 24:32]]
    Sf, shf = derived[:, 32:40], derived[:, 40:48]
    act(arow, prow[:, PR_ALOG:PR_ALOG + 16], AF.Exp, ["prow"], ["arow"])
    ts(arow, arow, -1.0, None, ALU.mult, None, ["arow"], ["arow"])
    act(esink, prow[:, PR_SINK:PR_SINK + 512], AF.Exp, ["prow"], ["esink"])
    dump("modcol", modcol, ["modcol"])

    def finish():
        if SCHED:
            P.schedule()
            print("sched est us per segment:", P.est, flush=True)
        P.finalize(st)
        keys = [k for k in P.dma_sem if isinstance(k, str) and (k.startswith("dbg_") or k.startswith("out"))]
        with nc.Block() as block:
            P.emit(block, final_dma_keys=keys)
        st.close()
        return nc

    if stop_after < 1:
        return finish()
    R1 = Alloc(24 * KB, 118 * KB)
    qT = R1.get(4 * L, BF16).rearrange("p (c t) -> p c t", c=4)
    kd = R1.get(2 * (L + LC), BF16).rearrange("p (g t) -> p g t", g=2)
    vtok = R1.get(18 * 128, BF16).rearrange("p (t c) -> p t c", t=18)
    sz = R1.get(16 * 512, BF16).rearrange("p (t c) -> p t c", t=16)
    xstok = R1.get(18 * 512, BF16).rearrange("p (t c) -> p t c", t=18)
    btok = R1.get(18 * 256, BF16).rearrange("p (t c) -> p t c", t=18)
    bT = R1.get(2 * (L + LC), BF16).rearrange("p (g t) -> p g t", g=2)
    cT = R1.get(2 * (L + LC), BF16).rearrange("p (g t) -> p g t", g=2)
    dtt = R1.get(18 * 16).rearrange("p (t c) -> p t c", t=18)
    dtab = R1.get(18 * 16, BF16).rearrange("p (t c) -> p t c", t=18)

    T1 = Alloc(118 * KB, 207 * KB)
    winb = T1.get(8 * INC, BF16).rearrange("p (k n) -> p k n", k=8)
    ropew = [T1.get(2 * 512, BF16).rearrange("p (s t) -> p s t", s=2) for _ in range(2)]
    hTw = [T1.get(8 * 514, BF16).rearrange("p (k n) -> p k n", k=8) for _ in range(2)]
    xt = [T1.get(1024) for _ in range(2)]
    xnb = [T1.get(1024, BF16)] * 2
    stat = T1.get(8)
    ustage = [T1.get(516) for _ in range(2)]
    tconv = [T1.get(512) for _ in range(2)]
    xbcs = T1.get(4 * 512, BF16).rearrange("p (c n) -> p c n", c=4)
    sqb = T1.get(512, BF16)
    qgb = T1.get(512, BF16)
    rstd = T1.get(512)
    t1 = T1.get(512)
    t2 = T1.get(512)
    dtmp = T1.get(16)
    for k in range(8):
        P.dma("pool", winb[:, k, :], win_h[:, k * INC:(k + 1) * INC], writes=["winb%d" % k], key="winb%d" % k)

    windows = [(False, 512 * w, 512) for w in range(4)] + [(True, 0, 256)]
    xtile_n = [0]
    for wi, (is_ctx, a0, nown) in enumerate(windows):
        hT = hTw[wi % 2]
        hk = "hT%d" % (wi % 2)
        src = ctx_h if is_ctx else x_h
        seqlen = LC if is_ctx else L
        r = 1 if is_ctx else 0
        rope, rk = ropew[wi % 2], "rope%d" % (wi % 2)
        if not is_ctx:
            P.dma("pool", rope, rope_h.rearrange("p (s t) -> p s t", s=2)[:, :, a0:a0 + 512], writes=[rk], key=rk)
        lo, hi = a0 - 1, a0 + nown + 1
        if lo < 0:
            P.op("pool", lambda e, hT=hT: e.memset(hT[:, :, 0:1], 0.0), [], [hk])
        if hi > seqlen:
            P.op("pool", lambda e, hT=hT, c=nown + 1: e.memset(hT[:, :, c:c + 1], 0.0), [], [hk])
        row = max(lo, 0)
        rend = min(hi, seqlen)
        while row < rend:
            n = min(128, rend - row)
            col0 = row - lo
            i = xtile_n[0] % 2
            xtile_n[0] += 1
            xk, nk = "xt%d" % i, "xnb0"
            P.dma("sp", xt[i][0:n, :], src[row:row + n, :], writes=[xk], key=xk)
            act(xnb[i][0:n, :], xt[i][0:n, :], AF.Square, [xk], [nk, "stat"], accum=stat[0:n, 0:1])
            act(stat[0:n, 1:2], stat[0:n, 0:1], AF.Ln, ["stat"], ["stat"], bias=EPS, scale=1.0 / D)
            act(stat[0:n, 2:3], stat[0:n, 1:2], AF.Exp, ["stat"], ["stat"], scale=-0.5)
            ts(xnb[i][0:n, :], xt[i][0:n, :], stat[0:n, 2:3], None, ALU.mult, None, [xk, "stat"], [nk])
            for k in range(8):
                tr(pbb[0][:, k * 128:k * 128 + n], xnb[i][0:n, k * 128:(k + 1) * 128], ident[0:n, 0:n], [nk, "cstb"], ["pb0"])
            src3 = pbb[0][:, 0:1024].rearrange("p (k n) -> p k n", k=8)[:, :, 0:n]
            dst3 = hT[:, :, col0:col0 + n]
            tt(dst3, src3, bc_last(Sa[r], n), ALU.mult, ["pb0", "derived"], [hk])
            tt(dst3, dst3, bc_last(sha[r], n), ALU.add, [hk, "derived"], [hk])
            row += n
        if wi == 0:
            dump("hT0", hT.rearrange("p k n -> p (k n)"), [hk])
        tok0 = (L if is_ctx else 0) + a0
        chunks = ([] if is_ctx else [("q", c) for c in range(4)]) + [("k", g) for g in range(2)]
        for ci, (kind, c) in enumerate(chunks):
            pbk = "pb%d" % (1 + ci % 2)
            pt = pb[1 + ci % 2]
            if kind == "q":
                for k in range(8):
                    mm(pt[:, 0:nown], winb[:, k, c * 128:(c + 1) * 128], hT[:, k, 1:1 + nown], k == 0, k == 7, ["winb%d" % k, hk], [pbk])
                gcol = pcol[:, PC_GQ:PC_GQ + 1]
            else:
                for half in range(2):
                    for k in range(8):
                        mm(pt[half * 64:(half + 1) * 64, 0:nown], winb[:, k, 512 + c * 64:512 + (c + 1) * 64], hT[:, k, 1:1 + nown],
                           k == 0, k == 7, ["winb%d" % k, hk], [pbk], tp=(0, 64 * half))
                gcol = pcol[:, PC_GK:PC_GK + 1]
            act(sqb[:, 0:nown], pt[:, 0:nown], AF.Square, [pbk], ["sqb"])
            act(qgb[:, 0:nown], pt[:, 0:nown], AF.Copy, [pbk, "pcol"], ["qgb"], scale=gcol)
            mm(pb[3][:, 0:nown], bones, sqb[:, 0:nown], True, True, ["cstb", "sqb"], ["pb3"])
            act(rstd[:, 0:nown], pb[3][:, 0:nown], AF.Ln, ["pb3"], ["rstd"], bias=EPS, scale=1.0 / 64)
            act(rstd[:, 0:nown], rstd[:, 0:nown], AF.Exp, ["rstd"], ["rstd"], scale=-0.5)
            dst = qT[:, c, a0:a0 + nown] if kind == "q" else kd[:, c, tok0:tok0 + nown]
            dk = "qT" if kind == "q" else "kd"
            if is_ctx:
                tt(dst, qgb[:, 0:nown], rstd[:, 0:nown], ALU.mult, ["qgb", "rstd"], [dk])
            else:
                mm(pb[4][:, 0:nown], rtm, qgb[:, 0:nown], True, True, ["cstb", "qgb"], ["pb4"])
                tt(t1[:, 0:nown], qgb[:, 0:nown], rope[:, 0, 0:nown], ALU.mult, ["qgb", rk], ["t1"])
                tt(t2[:, 0:nown], pb[4][:, 0:nown], rope[:, 1, 0:nown], ALU.mult, ["pb4", rk], ["t2"])
                tt(t1[:, 0:nown], t1[:, 0:nown], t2[:, 0:nown], ALU.add, ["t1", "t2"], ["t1"])
                tt(dst, t1[:, 0:nown], rstd[:, 0:nown], ALU.mult, ["t1", "rstd"], [dk])
        ncol = nown + 2
        for c in range(8):
            pbk = "pb%d" % (1 + c % 2)
            pt = pb[1 + c % 2]
            us, uk = ustage[c % 2], "us%d" % (c % 2)
            tc_, tk = tconv[c % 2], "tc%d" % (c % 2)
            w0c = 1280 + c * 128
            nmain = min(512, ncol)
            for k in range(8):
                mm(pt[:, 0:nmain], winb[:, k, w0c:w0c + 128], hT[:, k, 0:nmain], k == 0, k == 7, ["winb%d" % k, hk], [pbk])
            act(us[:, 0:nmain], pt[:, 0:nmain], AF.Copy, [pbk], [uk])
            if ncol > 512:
                for k in range(8):
                    mm(pb[7][:, 0:ncol - 512], winb[:, k, w0c:w0c + 128], hT[:, k, 512:ncol], k == 0, k == 7, ["winb%d" % k, hk], ["pb7"])
                act(us[:, 512:ncol], pb[7][:, 0:ncol - 512], AF.Copy, ["pb7"], [uk])
            wv = pcol[:, PC_SCW + 3 * c:PC_SCW + 3 * c + 3]
            bv = pcol[:, PC_SCB + c:PC_SCB + c + 1]
            ts(tc_[:, 0:nown], us[:, 1:1 + nown], wv[:, 1:2], bv, ALU.mult, ALU.add, [uk, "pcol"], [tk])
            stt(tc_[:, 0:nown], us[:, 0:nown], wv[:, 0:1], tc_[:, 0:nown], ALU.mult, ALU.add, [uk, tk, "pcol"], [tk])
            stt(tc_[:, 0:nown], us[:, 2:2 + nown], wv[:, 2:3], tc_[:, 0:nown], ALU.mult, ALU.add, [uk, tk, "pcol"], [tk])
            if c < 4:
                act(xbcs[:, c, 0:nown], tc_[:, 0:nown], AF.Silu, [tk], ["xbcs%d" % c])
            elif c < 6:
                act(bT[:, c - 4, tok0:tok0 + nown], tc_[:, 0:nown], AF.Silu, [tk], ["bT"])
            else:
                act(cT[:, c - 6, tok0:tok0 + nown], tc_[:, 0:nown], AF.Silu, [tk], ["cT"])
        for j in range(nown // 128):
            tile = tok0 // 128 + j
            cs = slice(j * 128, (j + 1) * 128)
            for c in range(4):
                tr(pbb[5][:, c * 128:(c + 1) * 128], xbcs[:, c, cs], ident, ["xbcs%d" % c, "cstb"], ["pb5"])
            for g in range(2):
                tr(pbb[5][:, 512 + g * 128:512 + (g + 1) * 128], bT[:, g, tok0 + j * 128:tok0 + (j + 1) * 128], ident, ["bT", "cstb"], ["pb5"])
            cp(xstok[:, tile, :], pbb[5][:, 0:512], ["pb5"], ["xstok"])
            cp(btok[:, tile, :], pbb[5][:, 512:768], ["pb5"], ["btok"])
            hs = hT[:, :, 1 + j * 128:1 + (j + 1) * 128]
            for k in range(8):
                mm(pb[6][:, 0:128], hs[:, k, :], winb[:, k, 640:768], k == 0, k == 7, [hk, "winb%d" % k], ["pb6"])
            for k in range(8):
                mm(pb[6][:, 128:144], hs[:, k, :], winb[:, k, 2304:2320], k == 0, k == 7, [hk, "winb%d" % k], ["pb6"])
            act(vtok[:, tile, :], pb[6][:, 0:128], AF.Copy, ["pb6"], ["vtok"])
            tt(dtmp, pb[6][:, 128:144], prow[:, PR_DTB:PR_DTB + 16], ALU.add, ["pb6", "prow"], ["dtmp"])
            act(dtmp, dtmp, AF.Exp, ["dtmp"], ["dtmp"])
            act(dtt[:, tile, :], dtmp, AF.Ln, ["dtmp"], ["dtt"], bias=1.0)
            tt(dtab[:, tile, :], dtt[:, tile, :], arow, ALU.mult, ["dtt", "arow"], ["dtab"])
            if not is_ctx:
                for k in range(8):
                    mm(pb[4][:, :], hs[:, k, :], winb[:, k, 768:1280], k == 0, k == 7, [hk, "winb%d" % k], ["pb4"])
                act(sz[:, tile, :], pb[4][:, :], AF.Silu, ["pb4"], ["sz"])
    dump("qT", qT.rearrange("p c t -> p (c t)"), ["qT"])
    dump("kd", kd.rearrange("p g t -> p (g t)"), ["kd"])
    dump("vtok", vtok.rearrange("p t c -> p (t c)"), ["vtok"])
    dump("sz", sz.rearrange("p t c -> p (t c)"), ["sz"])
    dump("xstok", xstok.rearrange("p t c -> p (t c)"), ["xstok"])
    dump("btok", btok.rearrange("p t c -> p (t c)"), ["btok"])
    dump("cT", cT.rearrange("p g t -> p (g t)"), ["cT"])
    dump("dtt", dtt.rearrange("p t c -> p (t c)"), ["dtt"])

    if stop_after < 2:
        return finish()
    P.barrier()
    T2 = Alloc(118 * KB, 207 * KB)
    ybw = T2.get(16 * 512, BF16).rearrange("p (t c) -> p t c", t=16)
    yT = T2.get(8 * L, BF16).rearrange("p (c t) -> p c t", c=8)

    T3 = Alloc(T2.o, 207 * KB)
    pT = [T3.get(5 * 512, BF16).rearrange("p (b n) -> p b n", b=5) for _ in range(2)]
    rsum = T3.get(256)
    rhs1 = [T3.get(1024, BF16) for _ in range(2)]
    nacol = [T3.get(8) for _ in range(2)]
    eacol = [T3.get(8) for _ in range(2)]
    dend = [T3.get(8) for _ in range(2)]
    cdec = [T3.get(8) for _ in range(2)]
    lt = [T3.get(1024, BF16) for _ in range(2)]
    xdt = [T3.get(512, BF16) for _ in range(2)]
    xw = [T3.get(512, BF16) for _ in range(2)]
    mt = T3.get(1024, BF16)
    yoffs = T3.get(512)
    ytmp = T3.get(512)
    gst = T3.get(8)
    ynb = T3.get(512, BF16)
    state = T3.get(512)
    stateb = T3.get(512, BF16)
    wmp = T3.get(8 * 256, BF16).rearrange("p (k n) -> p k n", k=8)
    bmp = T3.get(256)
    mrp = T3.get(256)

    def mod_piece(q):
        blk, off = q // 4, (q % 4) * 256
        P.dma("pool", wmp, wmod_h[blk].rearrange("p (k n) -> p k n", k=8)[:, :, off:off + 256], writes=["wmp"], key="wmp")
        P.dma("sp", bmp[0:2, :], bm_h[:, q * 256:(q + 1) * 256], writes=["bmp"], key="bmp")
        for k in range(8):
            mm(pb[6][0:2, 0:256], ccs3[:, k, :], wmp[:, k, :], k == 0, k == 7, ["ccs", "wmp"], ["pb6"])
        tt(mrp[0:2, :], pb[6][0:2, 0:256], bmp[0:2, :], ALU.add, ["pb6", "bmp"], ["mrp"])
        for j in range(2):
            tr(pb[6][:, 2 * j:2 * j + 2], mrp[0:2, j * 128:(j + 1) * 128], identf[0:2, 0:2], ["mrp", "identf"], ["pb6"])
        cp(modcol[:, q * 4:q * 4 + 4], pb[6][:, 0:4], ["pb6"], ["modcol"])
        if blk in (2, 5):
            dst = ga_a_bc if blk == 2 else ga_f_bc
            mm(pb[6][:, 0:256], onesf[0:1, 0:128], mrp[0:1, :], True, True, ["onesf", "mrp"], ["pb6"])
            cp(dst[:, off:off + 256], pb[6][:, 0:256], ["pb6"], ["gabc"])

    def att_unit(qt, g):
        qs = slice(qt * 128, (qt + 1) * 128)
        idx = (qt * 2 + g) % 2
        pTt, pk = pT[idx], "pT%d" % idx
        blocks = []
        if qt > 0:
            blocks.append((qt - 1, amp))
        blocks.append((qt, None))
        if qt < NT - 1:
            blocks.append((qt + 1, amn))
        blocks += [(16, None), (17, None)]
        for bi, (kt, mask) in enumerate(blocks):
            for r4 in range(4):
                par = r4 % 2
                pair = r4 // 2
                c = 2 * g + pair
                hp = slice(64 * par, 64 * par + 64)
                mm(pb[par][:, pair * 128:(pair + 1) * 128], kd[hp, g, kt * 128:(kt + 1) * 128], qT[hp, c, qs],
                   True, True, ["kd", "qT"], ["sc0"], tp=(64 * par, 0))
            src = pbig[:, 0:1024].rearrange("p (b n) -> p b n", b=2)[:, :, 0:256]
            act(pTt[:, bi, :].rearrange("p (b n) -> p b n", b=2), src, AF.Exp, ["sc0"], [pk], scale=0.125)
            if mask is not None:
                pv = pTt[:, bi, :].rearrange("p (q i) -> p q i", q=4)
                tt(pv, pv, bc_mid(mask, 4), ALU.mult, [pk, "cstb"], [pk], eng="pool")
        nb = len(blocks)
        for which in range(2):
            for bi, (kt, mask) in enumerate(blocks):
                for par in range(2):
                    op_ = slice(64 * par, 64 * par + 64)
                    lhs = vtok[:, kt, g * 64:(g + 1) * 64] if which == 0 else onesb[:, 0:64]
                    P.op("pe", lambda e, o_=pb[2][op_, which * 256:(which + 1) * 256], l_=lhs, r_=pTt[:, bi, par * 256:(par + 1) * 256],
                         s_=(bi == 0), t_=(bi == nb - 1), p_=par: e.matmul(o_, l_, r_, start=s_, stop=t_, tile_position=(0, 64 * p_)),
                         ["vtok", "cstb", pk], ["pb2"], cost=256 / 1.92 / 2 + 6, glue=not (bi == 0 and par == 0))
        for pair in range(2):
            act(rsum[:, pair * 128:(pair + 1) * 128], pb[2][:, 256 + pair * 128:256 + (pair + 1) * 128], AF.Ln, ["pb2", "esink"], ["rsum"],
                bias=esink[:, g * 256 + pair * 128:g * 256 + pair * 128 + 1])
        act(rsum, rsum, AF.Exp, ["rsum"], ["rsum"], scale=-1.0)
        tt(yT[:, 2 * g:2 * g + 2, qs], pb[2][:, 0:256].rearrange("p (a i) -> p a i", a=2),
           rsum.rearrange("p (a i) -> p a i", a=2), ALU.mult, ["pb2", "rsum"], ["yT"])

    ssm_n = [0]

    def ssm_unit(direction, tile):
        pp = ssm_n[0] % 2
        ssm_n[0] += 1
        sfx = "_%d" % pp
        rhs1_, nacol_, eacol_, dend_, cdec_, lt_, xdt_, xw_ = rhs1[pp], nacol[pp], eacol[pp], dend[pp], cdec[pp], lt[pp], xdt[pp], xw[pp]
        tri = trib if direction else trif
        smask = smb if direction else smf
        dsl = slice(8 * direction, 8 * direction + 8)
        iend = 0 if direction else 127
        is_ctx = tile >= 16
        tsl = slice(tile * 128, (tile + 1) * 128)
        CB, X = pb[3], pb[4]
        tt(rhs1_.rearrange("p (h i) -> p h i", h=8), bc_mid(tri, 8), bc_last(dtab[:, tile, dsl], 128), ALU.mult,
           ["cstb", "dtab"], ["rhs1" + sfx], eng="pool")
        mm(pb[5][:, 0:8], tri, dtab[:, tile, dsl], True, True, ["cstb", "dtab"], ["pb5"])
        ts(nacol_, pb[5][:, 0:8], -1.0, None, ALU.mult, None, ["pb5"], ["nacol" + sfx])
        if not is_ctx:
            act(eacol_, pb[5][:, 0:8], AF.Exp, ["pb5"], ["eacol" + sfx])
        for half in range(2):
            hs_ = slice(half * 512, (half + 1) * 512)
            mm(CB, onesb, rhs1_[:, hs_], True, False, ["cstb", "rhs1" + sfx], ["pb3"])
            mm(CB, ident, smask[:, hs_], False, True, ["cstb"], ["pb3"])
            totv = CB.rearrange("p (h i) -> p h i", h=4)[:, :, iend]
            tt(dend_[:, half * 4:half * 4 + 4], totv, nacol_[:, half * 4:half * 4 + 4], ALU.add, ["pb3", "nacol" + sfx], ["dend" + sfx])
            act(cdec_[:, half * 4:half * 4 + 4], totv, AF.Exp, ["pb3"], ["cdec" + sfx])
            if not is_ctx:
                for hh in range(4):
                    h = half * 4 + hh
                    act(lt_[:, h * 128:(h + 1) * 128], CB[:, hh * 128:(hh + 1) * 128], AF.Exp,
                        ["pb3", "nacol" + sfx], ["lt" + sfx], bias=nacol_[:, h:h + 1])
        act(dend_, dend_, AF.Exp, ["dend" + sfx], ["dend" + sfx])
        tt(xdt_.rearrange("p (h c) -> p h c", h=8), xstok[:, tile, :].rearrange("p (h c) -> p h c", h=8),
           bc_last(dtt[:, tile, dsl], 64), ALU.mult, ["xstok", "dtt"], ["xdt" + sfx], eng="pool")
        tt(xw_.rearrange("p (h c) -> p h c", h=8), xdt_.rearrange("p (h c) -> p h c", h=8), bc_last(dend_, 64), ALU.mult,
           ["xdt" + sfx, "dend" + sfx], ["xw" + sfx], eng="pool")
        if not is_ctx:
            for g in range(2):
                mm(pb[6][:, g * 128:(g + 1) * 128], bT[:, g, tsl], cT[:, g, tsl], True, True, ["bT", "cT"], ["pb6"])
            tt(mt.rearrange("p (g r i) -> p g r i", g=2, r=4), lt_.rearrange("p (g r i) -> p g r i", g=2, r=4),
               pb[6][:, 0:256].rearrange("p (g i) -> p g i", g=2).unsqueeze(2).to_broadcast([128, 2, 4, 128]),
               ALU.mult, ["lt" + sfx, "pb6"], ["mt"])
            for g in range(2):
                mm(X[:, g * 256:(g + 1) * 256], cT[:, g, tsl], stateb[:, g * 256:(g + 1) * 256], True, True,
                   ["cT", "stateb"], ["pb4"])
            tt(yoffs.rearrange("p (h c) -> p h c", h=8), X.rearrange("p (h c) -> p h c", h=8), bc_last(eacol_, 64),
               ALU.mult, ["pb4", "eacol" + sfx], ["yoffs"])
            for h in range(8):
                mm(X[:, h * 64:(h + 1) * 64], mt[:, h * 128:(h + 1) * 128], xdt_[:, h * 64:(h + 1) * 64], True, True,
                   ["mt", "xdt" + sfx], ["pb4"])
        for g in range(2):
            mm(pb[7][:, g * 256:(g + 1) * 256], btok[:, tile, g * 128:(g + 1) * 128], xw_[:, g * 256:(g + 1) * 256], True, True,
               ["btok", "xw" + sfx], ["pb7"])
        if not is_ctx:
            if direction:
                tt(ytmp, X, yoffs, ALU.add, ["pb4", "yoffs"], ["ytmp"])
                tt(yoffs.rearrange("p (h c) -> p h c", h=8), xstok[:, tile, :].rearrange("p (h c) -> p h c", h=8),
                   bc_last(prow[:, PR_DSK:PR_DSK + 8], 64), ALU.mult, ["xstok", "prow"], ["yoffs"], eng="pool")
                tt(ybw[:, tile, :], ytmp, yoffs, ALU.add, ["ytmp", "yoffs"], ["ybw"])
            else:
                tt(ytmp, X, yoffs, ALU.add, ["pb4", "yoffs"], ["ytmp"])
        tt(state.rearrange("p (h c) -> p h c", h=8), state.rearrange("p (h c) -> p h c", h=8), bc_last(cdec_, 64), ALU.mult,
           ["state", "cdec" + sfx], ["state"])
        tt(state, state, pb[7], ALU.add, ["state", "pb7"], ["state"])
        act(stateb, state, AF.Copy, ["state"], ["stateb"])
        if not is_ctx and not direction:
            tt(ytmp, ytmp, ybw[:, tile, :], ALU.add, ["ytmp", "ybw"], ["ytmp"])
            tt(ytmp, ytmp, sz[:, tile, :], ALU.mult, ["ytmp", "sz"], ["ytmp"])
            for g in range(2):
                act(mt[:, g * 256:(g + 1) * 256], ytmp[:, g * 256:(g + 1) * 256], AF.Square, ["ytmp"], ["mt", "gst"],
                    accum=gst[:, g:g + 1])
            act(gst[:, 2:4], gst[:, 0:2], AF.Ln, ["gst"], ["gst"], bias=EPS, scale=1.0 / 256)
            act(gst[:, 4:6], gst[:, 2:4], AF.Exp, ["gst"], ["gst"], scale=-0.5)
            tt(ytmp.rearrange("p (g c) -> p g c", g=2), ytmp.rearrange("p (g c) -> p g c", g=2), bc_last(gst[:, 4:6], 256),
               ALU.mult, ["ytmp", "gst"], ["ytmp"])
            tt(ynb, ytmp, prow[:, PR_GSSM:PR_GSSM + 512], ALU.mult, ["ytmp", "prow"], ["ynb"], eng="pool")
            for c in range(4):
                tr(pbb[7][:, c * 128:(c + 1) * 128], ynb[:, c * 128:(c + 1) * 128], ident, ["ynb", "cstb"], ["pb7"])
            cp(yT[:, 4:8, tsl], pbb[7][:, 0:512].rearrange("p (c i) -> p c i", c=4), ["pb7"], ["yT"])

    ssm_units = []
    for direction in (1, 0):
        ctx_tiles = [17, 16] if direction else [16, 17]
        lat_tiles = list(range(15, -1, -1)) if direction else list(range(16))
        ssm_units.append(("reset", direction))
        for tile in ctx_tiles + lat_tiles:
            ssm_units.append((direction, tile))
    att_units = [(qt, g) for qt in range(NT) for g in range(2)]
    ai = 0
    pieces = list(range(8, 24))
    nunit = [0]
    for u in ssm_units:
        nunit[0] += 1
        if pieces and u[0] != "reset":
            mod_piece(pieces.pop(0))
        if u[0] == "reset":
            P.op("pool", lambda e: e.memset(state, 0.0), [], ["state"])
            P.op("pool", lambda e: e.memset(stateb, 0.0), [], ["stateb"])
            continue
        if ai < len(att_units):
            if not skip_att:
                att_unit(*att_units[ai])
            ai += 1
        if not skip_ssm:
            ssm_unit(*u)
    while ai < len(att_units):
        if not skip_att:
            att_unit(*att_units[ai])
        ai += 1
    assert not pieces
    stt(derived[:, 32:40], modc3[:, 32:40, 0], 1.0, pcol[:, PC_GFFN:PC_GFFN + 8], ALU.add, ALU.mult, ["modcol", "pcol"], ["derived2"])
    cp(derived[:, 40:48], modc3[:, 24:32, 0], ["modcol"], ["derived2"])
    dump("yTa", yT[:, 0:4, :].rearrange("p c t -> p (c t)"), ["yT"])
    dump("yTs", yT[:, 4:8, :].rearrange("p c t -> p (c t)"), ["yT"])

    if stop_after < 4:
        return finish()
    P.barrier()
    xnew = V(24 * KB, 64 * KB).rearrange("p (t c) -> p t c", t=16)
    T5 = Alloc(88 * KB, 118 * KB)
    woutb = T5.get(8 * 1024, BF16).rearrange("p (k n) -> p k n", k=8)
    xt4 = [T5.get(1024) for _ in range(2)]
    P.dma("pool", woutb.rearrange("p k n -> p (k n)"), wout_h, writes=["woutb"], key="woutb")
    for t in range(NT):
        i = t % 2
        xk = "x4_%d" % i
        P.dma("sp", xt4[i], x_h[t * 128:(t + 1) * 128, :], writes=[xk], key=xk)
        for half in range(2):
            pt, pk = pb[(2 * t + half) % 4], "pb%d" % ((2 * t + half) % 4)
            for k in range(8):
                mm(pt[:, :], yT[:, k, t * 128:(t + 1) * 128], woutb[:, k, half * 512:(half + 1) * 512], k == 0, k == 7,
                   ["yT", "woutb"], [pk])
            hsl = slice(half * 512, (half + 1) * 512)
            tt(xnew[:, t, hsl], pt[:, :], ga_a_bc[:, hsl], ALU.mult, [pk, "gabc"], ["xnew%d" % t])
            tt(xnew[:, t, hsl], xnew[:, t, hsl], xt4[i][:, hsl], ALU.add, ["xnew%d" % t, xk], ["xnew%d" % t])
    dump("xnew", xnew.rearrange("p t c -> p (t c)"), ["xnew%d" % t for t in range(NT)])

    if stop_after < 5:
        return finish()
    S6 = Alloc(166 * KB, 207 * KB)
    U6 = Alloc(88 * KB, 166 * KB)
    h2w = [S6.get(8 * 514, BF16).rearrange("p (k n) -> p k n", k=8) for _ in range(2)]
    wupb = [S6.get(2 * 8 * 128, BF16).rearrange("p (s k n) -> p s k n", s=2, k=8) for _ in range(3)]
    sg5 = S6.get(512)
    xn5 = [S6.get(1024, BF16)] * 2
    stat5 = S6.get(8)
    ost = [S6.get(512) for _ in range(2)]
    wdnb = U6.get(22 * 1024, BF16).rearrange("p (k n) -> p k n", k=22)
    actT = U6.get(22 * 512, BF16).rearrange("p (k n) -> p k n", k=22)
    tc5 = [U6.get(512) for _ in range(6)] + [S6.get(512) for _ in range(2)]
    P.op("sp", lambda e: e.nop(), [], ["woutb", "yT", "x4_0", "x4_1", "actT"] + ["wdnb%d" % k for k in range(22)]
         + ["tc5_%d" % i for i in range(6)], cost=50.0)
    for k in range(22):
        P.dma("pool", wdnb[:, k, :], wdn_h[:, k * D:(k + 1) * D], writes=["wdnb%d" % k], key="wdnb%d" % k)
    ost_n = [0]

    def issue_wup(n_):
        j_ = n_ % 22
        wi_ = n_ % 3
        wk_ = "wup%d" % wi_
        P.dma("pool", wupb[wi_].rearrange("p s k n -> p (s k n)"), wup_h[j_], writes=[wk_ + "_0", wk_ + "_1"], key=wk_)

    issue_wup(0)
    issue_wup(1)
    for w in range(4):
        a0 = 512 * w
        h2 = h2w[w % 2]
        hk = "h2_%d" % (w % 2)
        lo, hi = a0 - 1, a0 + 513
        if lo < 0:
            P.op("pool", lambda e, h2=h2: e.memset(h2[:, :, 0:1], 0.0), [], [hk])
        if hi > L:
            P.op("pool", lambda e, h2=h2: e.memset(h2[:, :, 513:514], 0.0), [], [hk])
        first_t = 0 if w == 0 else 4 * w + 1
        last_t = min(4 * w + 4, NT - 1)
        for t in range(first_t, last_t + 1):
            wt, jt_ = t // 4, t % 4
            hb_ = h2w[wt % 2]
            hkk = "h2_%d" % (wt % 2)
            nk = "xn5_0"
            xk = "xnew%d" % t
            act(xn5[0], xnew[:, t, :], AF.Square, [xk], [nk, "stat5"], accum=stat5[:, 0:1])
            act(stat5[:, 1:2], stat5[:, 0:1], AF.Ln, ["stat5"], ["stat5"], bias=EPS, scale=1.0 / D)
            act(stat5[:, 2:3], stat5[:, 1:2], AF.Exp, ["stat5"], ["stat5"], scale=-0.5)
            ts(xn5[0], xnew[:, t, :], stat5[:, 2:3], None, ALU.mult, None, [xk, "stat5"], [nk])
            for k in range(8):
                tr(pbb[0][:, k * 128:(k + 1) * 128], xn5[0][:, k * 128:(k + 1) * 128], ident, [nk, "cstb"], ["pb0"])
            src_all = pbb[0][:, 0:1024].rearrange("p (k n) -> p k n", k=8)
            targets = [(hb_, hkk, 0, 128, 1 + 128 * jt_)]
            if jt_ == 0 and wt > 0:
                targets.append((h2w[(wt - 1) % 2], "h2_%d" % ((wt - 1) % 2), 0, 1, 513))
            if jt_ == 3 and wt < 3:
                targets.append((h2w[(wt + 1) % 2], "h2_%d" % ((wt + 1) % 2), 127, 128, 0))
            for (hbuf, hkey, r0, r1, col0) in targets:
                n = r1 - r0
                src3 = src_all[:, :, r0:r1]
                dst3 = hbuf[:, :, col0:col0 + n]
                tt(dst3, src3, bc_last(Sf, n), ALU.mult, ["pb0", "derived2"], [hkey])
                tt(dst3, dst3, bc_last(shf, n), ALU.add, [hkey, "derived2"], [hkey])
        for j in range(22):
            n_ = w * 22 + j
            if n_ + 2 < 88:
                issue_wup(n_ + 2)
            wi = n_ % 3
            wb = wupb[wi]
            tks = []
            for s in range(2):
                wk = "wup%d_%d" % (wi, s)
                bi = (2 * j + s) % 4
                pt, pk = pb[1 + bi], "pb%d" % (1 + bi)
                ti = (2 * n_ + s) % len(tc5)
                tc_, tk = tc5[ti], "tc5_%d" % ti
                tks.append((tc_, tk))
                hbi = (7, 5, 6)[(2 * n_ + s) % 3]
                hbk = "pb%d" % hbi
                hal = pb[hbi]
                for k in range(8):
                    mm(pt[:, :], wb[:, s, k, :], h2[:, k, 0:512], k == 0, k == 7, [wk, hk], [pk])
                for k in range(8):
                    mm(hal[:, 0:2], wb[:, s, k, :], h2[:, k, 512:514], k == 0, k == 7, [wk, hk], [hbk])
                ch = s * 22 + j
                wv = pcol[:, PC_FCW + 3 * ch:PC_FCW + 3 * ch + 3]
                bv = pcol[:, PC_FCB + ch:PC_FCB + ch + 1]
                act(tc_[:, 0:511], pt[:, 1:512], AF.Identity, [pk, "pcol"], [tk], bias=bv, scale=wv[:, 1:2])
                act(tc_[:, 511:512], hal[:, 0:1], AF.Identity, [hbk, "pcol"], [tk], bias=bv, scale=wv[:, 1:2])
                stt(tc_, pt[:, 0:512], wv[:, 0:1], tc_, ALU.mult, ALU.add, [pk, tk, "pcol"], [tk])
                stt(tc_[:, 0:510], pt[:, 2:512], wv[:, 2:3], tc_[:, 0:510], ALU.mult, ALU.add, [pk, tk, "pcol"], [tk])
                stt(tc_[:, 510:512], hal[:, 0:2], wv[:, 2:3], tc_[:, 510:512], ALU.mult, ALU.add, [hbk, tk, "pcol"], [tk])
            (tca, tka), (tcg, tkg) = tks
            act(sg5, tcg, AF.Silu, [tkg], ["sg5"])
            tt(actT[:, j, :], tca, sg5, ALU.mult, [tka, "sg5"], ["actT"], eng="pool")
        for jt in range(4):
            t = a0 // 128 + jt
            for half in range(2):
                oi = ost_n[0] % 2
                ost_n[0] += 1
                ok = "ost%d" % oi
                pt, pk = pb[5 + half], "pb%d" % (5 + half)
                for k in range(22):
                    mm(pt[:, :], actT[:, k, jt * 128:(jt + 1) * 128], wdnb[:, k, half * 512:(half + 1) * 512], k == 0, k == 21,
                       ["actT", "wdnb%d" % k], [pk])
                hsl = slice(half * 512, (half + 1) * 512)
                tt(ost[oi], pt[:, :], ga_f_bc[:, hsl], ALU.mult, [pk, "gabc"], [ok])
                tt(ost[oi], ost[oi], xnew[:, t, hsl], ALU.add, [ok, "xnew%d" % t], [ok])
                P.dma("sp", out_h[t * 128:(t + 1) * 128, hsl], ost[oi], reads=[ok], key="out%d" % oi)

    return finish()


_NC_CACHE = {}


def make_in_maps(x, c, ctx, c_ctx, w_mod, b_mod, g_mix, w_in, g_q, g_k, sink, ssm_conv_w, ssm_conv_b,
                 a_log, dt_bias, d_skip, g_ssm, w_out, g_ffn, w_up, ffn_conv_w, ffn_conv_b, w_down):
    f = lambda a: np.ascontiguousarray(np.asarray(a, dtype=np.float32))
    cst = host_consts()
    rope = host_rope()
    bm2 = np.ascontiguousarray(np.tile(f(b_mod)[0][None, :], (2, 1)))
    c_ = np.ascontiguousarray
    wm = c_(f(w_mod)[0].reshape(8, 128, 6, 1024).transpose(2, 1, 0, 3).reshape(6, 128, 8 * 1024))
    wi = c_(f(w_in)[0].reshape(8, 128, INC).transpose(1, 0, 2).reshape(128, 8 * INC))
    wo = c_(f(w_out)[0].reshape(8, 128, D).transpose(1, 0, 2).reshape(128, 8 * D))
    wu = c_(f(w_up)[0].reshape(8, 128, 2, 22, 128).transpose(3, 1, 2, 0, 4).reshape(22, 128, 2 * 8 * 128))
    wd = c_(f(w_down)[0].reshape(22, 128, D).transpose(1, 0, 2).reshape(128, 22 * D))
    shared = {
        "w_mod": wm, "w_in": wi, "w_out": wo, "w_up": wu, "w_down": wd,
        "bmod2": bm2, "cst": cst, "rope": rope,
    }
    maps = []
    for b in range(8):
        pc, pr = host_params(f(c)[b], f(c_ctx), f(g_mix)[0], f(g_ffn)[0], f(g_q)[0], f(g_k)[0], f(ssm_conv_w)[0],
                             f(ssm_conv_b)[0], f(ffn_conv_w)[0], f(ffn_conv_b)[0], f(dt_bias)[0], f(a_log)[0],
                             f(d_skip)[0], f(g_ssm)[0], f(sink)[0])
        m = dict(shared)
        m.update({"x": f(x)[b], "ctx": f(ctx)[b], "pcol": pc, "prow": pr})
        maps.append(m)
    return maps


def kernel(**inputs):
    if "nc" not in _NC_CACHE:
        _NC_CACHE["nc"] = build_nc()
    nc = _NC_CACHE["nc"]
    maps = make_in_maps(**inputs)
    res = run_bass_kernel_spmd(nc, maps, core_ids=list(range(8)))
    return np.stack([np.asarray(r["out"], dtype=np.float32) for r in res.results], axis=0)
```
